# Optimizing a Trainium2 kernel written in Bass

```python
import math
import jax, jax.numpy as jnp
from jax import lax
import numpy as np

D_MODEL = 1024
BATCH = 4
SEQ = 8192
DEPTH = 2

MEM_LEN = 256
RMS_EPS = 1e-6
ROPE_THETA = 500000.0
ROPE_FRACTION = 4
Q_BLOCK = 128
MOBA_HEADS = 6
MOBA_DIM = 64
MOBA_BLOCK = 256
MOBA_TOPK = 3
MOBA_Q_BLOCK = 64
DIFF_HEADS = 4
DIFF_QK_DIM = 32
DIFF_V_DIM = 64
SB_HEADS = 6
SB_DIM = 64
XATTN_HEADS = 4
XATTN_DIM = 128

MOBA_W = MOBA_HEADS * MOBA_DIM
DIFF_QK_W = DIFF_HEADS * 2 * DIFF_QK_DIM
DIFF_W = DIFF_HEADS * DIFF_V_DIM
SB_W = SB_HEADS * SB_DIM
D_MIX = MOBA_W + DIFF_W + SB_W
IN_SPLIT_SIZES = (MOBA_W, MOBA_W, MOBA_W, MOBA_W,
                  DIFF_QK_W, DIFF_QK_W, DIFF_W, DIFF_W,
                  SB_W, SB_W, SB_W, SB_W)
D_IN = 4 * MOBA_W + 2 * DIFF_QK_W + 2 * DIFF_W + 4 * SB_W
XATTN_W = XATTN_HEADS * XATTN_DIM

kernel_name = "hybrid_moba_diff_stickbreak_memxattn"


def rms_norm(x, g):
    xf = x.astype(jnp.float32)
    y = xf * lax.rsqrt(jnp.mean(xf * xf, axis=-1, keepdims=True) + RMS_EPS)
    return (y * g.astype(jnp.float32)).astype(x.dtype)


def to_heads(t, n_heads, d):
    b, s, _ = t.shape
    return t.reshape(b, s, n_heads, d).transpose(0, 2, 1, 3)


def from_heads(t):
    b, h, s, d = t.shape
    return t.transpose(0, 2, 1, 3).reshape(b, s, h * d)


def rope_tables(positions, head_dim):
    rot = head_dim // ROPE_FRACTION
    inv = 1.0 / (ROPE_THETA ** (jnp.arange(0, rot, 2, dtype=jnp.float32) / rot))
    ang = positions.astype(jnp.float32)[..., None] * inv
    return jnp.cos(ang)[:, None], jnp.sin(ang)[:, None]


def apply_partial_rope(x, cos, sin):
    r2 = cos.shape[-1]
    c = cos.astype(x.dtype)
    s = sin.astype(x.dtype)
    x1 = x[..., :r2]
    x2 = x[..., r2:2 * r2]
    return jnp.concatenate([x1 * c - x2 * s, x2 * c + x1 * s, x[..., 2 * r2:]], axis=-1)


def sweep_blocks(fn, seq, qb):
    out = lax.map(fn, jnp.arange(seq // qb))
    n, b, h, _, d = out.shape
    return out.transpose(1, 2, 0, 3, 4).reshape(b, h, n * qb, d)


def moba_attention(q, k, v):
    b, h, s, d = q.shape
    nb = -(-s // MOBA_BLOCK)
    pad = nb * MOBA_BLOCK - s
    kp = jnp.pad(k, ((0, 0), (0, 0), (0, pad), (0, 0)))
    vp = jnp.pad(v, ((0, 0), (0, 0), (0, pad), (0, 0)))
    kb = kp.reshape(b, h, nb, MOBA_BLOCK, d)
    vb = vp.reshape(b, h, nb, MOBA_BLOCK, d)
    k_mean = jnp.mean(kb.astype(jnp.float32), axis=3)
    n_sel = min(MOBA_TOPK, nb)
    scale = d ** -0.5
    bi = jnp.arange(b)[:, None, None, None]
    hi = jnp.arange(h)[None, :, None, None]

    def block(i):
        q0 = i * MOBA_Q_BLOCK
        qi = lax.dynamic_slice_in_dim(q, q0, MOBA_Q_BLOCK, axis=2)
        own = q0 // MOBA_BLOCK
        tq = q0 + jnp.arange(MOBA_Q_BLOCK)
        gate = jnp.einsum('bhqd,bhnd->bhqn', qi.astype(jnp.float32), k_mean)
        gate = jnp.where(jnp.arange(nb) < own, gate, -jnp.inf)
        _, idx = lax.top_k(gate, n_sel)
        valid = idx < own
        ks = kb[bi, hi, idx]
        vs = vb[bi, hi, idx]
        s_sel = jnp.einsum('bhqd,bhqnld->bhqnl', qi, ks).astype(jnp.float32) * scale
        s_sel = jnp.where(valid[..., None], s_sel, -jnp.inf)
        s_sel = s_sel.reshape(b, h, MOBA_Q_BLOCK, n_sel * MOBA_BLOCK)
        ko = lax.dynamic_slice_in_dim(kp, own * MOBA_BLOCK, MOBA_BLOCK, axis=2)
        vo = lax.dynamic_slice_in_dim(vp, own * MOBA_BLOCK, MOBA_BLOCK, axis=2)
        s_own = jnp.einsum('bhqd,bhld->bhql', qi, ko).astype(jnp.float32) * scale
        to = own * MOBA_BLOCK + jnp.arange(MOBA_BLOCK)
        s_own = jnp.where(to[None, :] <= tq[:, None], s_own, -jnp.inf)
        p = jax.nn.softmax(jnp.concatenate([s_sel, s_own], axis=-1), axis=-1)
        p_sel = p[..., :n_sel * MOBA_BLOCK].reshape(b, h, MOBA_Q_BLOCK, n_sel, MOBA_BLOCK).astype(v.dtype)
        p_own = p[..., n_sel * MOBA_BLOCK:].astype(v.dtype)
        return (jnp.einsum('bhqnl,bhqnld->bhqd', p_sel, vs)
                + jnp.einsum('bhql,bhld->bhqd', p_own, vo))

    return sweep_blocks(block, s, MOBA_Q_BLOCK)


def diff_attention(q, k, v, lam, head_norm_g, lam_init):
    b, h2, s, dq = q.shape
    h = h2 // 2
    scale = dq ** -0.5
    kpos = jnp.arange(s)

    def block(i):
        q0 = i * Q_BLOCK
        qi = lax.dynamic_slice_in_dim(q, q0, Q_BLOCK, axis=2)
        sc = jnp.einsum('bhqd,bhkd->bhqk', qi, k).astype(jnp.float32) * scale
        mask = kpos[None, :] <= (q0 + jnp.arange(Q_BLOCK))[:, None]
        p = jax.nn.softmax(jnp.where(mask, sc, -jnp.inf), axis=-1).reshape(b, h, 2, Q_BLOCK, s)
        a = p[:, :, 0] - lam * p[:, :, 1]
        return jnp.einsum('bhqk,bhkd->bhqd', a.astype(v.dtype), v)

    o = sweep_blocks(block, s, Q_BLOCK)
    o = rms_norm(o, head_norm_g)
    return o * (1.0 - lam_init)


def stick_breaking_attention(q, k, v):
    b, h, s, d = q.shape
    scale = d ** -0.5
    kpos = jnp.arange(s)

    def block(i):
        q0 = i * Q_BLOCK
        qi = lax.dynamic_slice_in_dim(q, q0, Q_BLOCK, axis=2)
        z = jnp.einsum('bhqd,bhkd->bhqk', qi, k).astype(jnp.float32) * scale
        mask = kpos[None, :] < (q0 + jnp.arange(Q_BLOCK))[:, None]
        log_beta = jax.nn.log_sigmoid(z)
        log_1m_beta = jnp.where(mask, log_beta - z, 0.0)
        between = lax.cumsum(log_1m_beta, axis=3, reverse=True) - log_1m_beta
        w = jnp.where(mask, jnp.exp(log_beta + between), 0.0)
        return jnp.einsum('bhqk,bhkd->bhqd', w.astype(v.dtype), v)

    return sweep_blocks(block, s, Q_BLOCK)


def hybrid_mixer(hn, rope64, rope32, w_in, w_out, lq1, lk1, lq2, lk2, head_norm_g, lam_init):
    proj = hn @ w_in
    cuts = list(np.cumsum(IN_SPLIT_SIZES)[:-1])
    (mq, mk, mv, mg, dq_, dk_, dv_, dg, sq, sk, sv, sg) = jnp.split(proj, cuts, axis=-1)
    cos64, sin64 = rope64
    a_q = apply_partial_rope(to_heads(mq, MOBA_HEADS, MOBA_DIM), cos64, sin64)
    a_k = apply_partial_rope(to_heads(mk, MOBA_HEADS, MOBA_DIM), cos64, sin64)
    a_o = moba_attention(a_q, a_k, to_heads(mv, MOBA_HEADS, MOBA_DIM))
    cos32, sin32 = rope32
    b_q = apply_partial_rope(to_heads(dq_, 2 * DIFF_HEADS, DIFF_QK_DIM), cos32, sin32)
    b_k = apply_partial_rope(to_heads(dk_, 2 * DIFF_HEADS, DIFF_QK_DIM), cos32, sin32)
    f32 = jnp.float32
    lam = (jnp.exp(jnp.sum(lq1.astype(f32) * lk1.astype(f32)))
           - jnp.exp(jnp.sum(lq2.astype(f32) * lk2.astype(f32))) + lam_init)
    b_o = diff_attention(b_q, b_k, to_heads(dv_, DIFF_HEADS, DIFF_V_DIM), lam, head_norm_g, lam_init)
    c_o = stick_breaking_attention(to_heads(sq, SB_HEADS, SB_DIM), to_heads(sk, SB_HEADS, SB_DIM),
                                   to_heads(sv, SB_HEADS, SB_DIM))
    merged = jnp.concatenate([from_heads(a_o) * jax.nn.silu(mg),
                              from_heads(b_o).astype(hn.dtype) * jax.nn.silu(dg),
                              from_heads(c_o) * jax.nn.silu(sg)], axis=-1)
    return merged @ w_out


def memory_cross_attention(hn, mem_n, w_xq, w_xkv, w_xo):
    q = to_heads(hn @ w_xq, XATTN_HEADS, XATTN_DIM)
    k, v = jnp.split(mem_n @ w_xkv, 2, axis=-1)
    k = to_heads(k, XATTN_HEADS, XATTN_DIM)
    v = to_heads(v, XATTN_HEADS, XATTN_DIM)
    sc = jnp.einsum('bhsd,bhmd->bhsm', q, k).astype(jnp.float32) * (XATTN_DIM ** -0.5)
    p = jax.nn.softmax(sc, axis=-1).astype(v.dtype)
    o = jnp.einsum('bhsm,bhmd->bhsd', p, v)
    return from_heads(o) @ w_xo


def setup_inputs(seed: int = 0) -> dict:
    key = jax.random.key(seed)
    ks = jax.random.split(key, 20)
    f32 = jnp.float32

    def w(k, shape, fan_in):
        return jax.random.normal(k, shape, f32) * (fan_in ** -0.5)

    def gain(k, shape):
        return 1.0 + 0.02 * jax.random.normal(k, shape, f32)

    offsets = jax.random.randint(ks[2], (BATCH, 1), 0, 4096, dtype=jnp.int32)
    positions = offsets + jnp.arange(SEQ, dtype=jnp.int32)[None, :]
    return {
        "x": jax.random.normal(ks[0], (BATCH, SEQ, D_MODEL), f32),
        "mem": jax.random.normal(ks[1], (BATCH, MEM_LEN, D_MODEL), f32),
        "positions": positions,
        "attn_norm_g": gain(ks[3], (DEPTH, D_MODEL)),
        "w_in": w(ks[4], (DEPTH, D_MODEL, D_IN), D_MODEL),
        "w_out": w(ks[5], (DEPTH, D_MIX, D_MODEL), D_MIX),
        "diff_lambda_q1": 0.1 * jax.random.normal(ks[6], (DEPTH, DIFF_QK_DIM), f32),
        "diff_lambda_k1": 0.1 * jax.random.normal(ks[7], (DEPTH, DIFF_QK_DIM), f32),
        "diff_lambda_q2": 0.1 * jax.random.normal(ks[8], (DEPTH, DIFF_QK_DIM), f32),
        "diff_lambda_k2": 0.1 * jax.random.normal(ks[9], (DEPTH, DIFF_QK_DIM), f32),
        "diff_head_norm_g": gain(ks[10], (DEPTH, DIFF_V_DIM)),
        "xattn_norm_g": gain(ks[11], (DEPTH, D_MODEL)),
        "mem_norm_g": gain(ks[12], (DEPTH, D_MODEL)),
        "w_xq": w(ks[13], (DEPTH, D_MODEL, XATTN_W), D_MODEL),
        "w_xkv": w(ks[14], (DEPTH, D_MODEL, 2 * XATTN_W), D_MODEL),
        "w_xo": w(ks[15], (DEPTH, XATTN_W, D_MODEL), XATTN_W),
        "final_norm_g": gain(ks[16], (D_MODEL,)),
    }


def reference(x, mem, positions, attn_norm_g, w_in, w_out, diff_lambda_q1, diff_lambda_k1,
              diff_lambda_q2, diff_lambda_k2, diff_head_norm_g, xattn_norm_g, mem_norm_g,
              w_xq, w_xkv, w_xo, final_norm_g):
    rope64 = rope_tables(positions, MOBA_DIM)
    rope32 = rope_tables(positions, DIFF_QK_DIM)
    h = x
    for l in range(DEPTH):
        lam_init = 0.8 - 0.6 * math.exp(-0.3 * l)
        hn = rms_norm(h, attn_norm_g[l])
        h = h + hybrid_mixer(hn, rope64, rope32, w_in[l], w_out[l],
                             diff_lambda_q1[l], diff_lambda_k1[l], diff_lambda_q2[l], diff_lambda_k2[l],
                             diff_head_norm_g[l], lam_init)
        h = h + memory_cross_attention(rms_norm(h, xattn_norm_g[l]), rms_norm(mem, mem_norm_g[l]),
                                       w_xq[l], w_xkv[l], w_xo[l])
    return rms_norm(h, final_norm_g)
```

```python
import math
import numpy as np
import ml_dtypes
import concourse.bass as bass
import concourse.mybir as mybir
from concourse.bass_utils import run_bass_kernel_spmd

F32 = mybir.dt.float32
BF16 = mybir.dt.bfloat16
I32 = mybir.dt.int32
AF = mybir.ActivationFunctionType
ALU = mybir.AluOpType
AX = mybir.AxisListType

D = 1024
SEQ = 8192
BATCH = 4
MEM = 256
EPS = 1e-6
THETA = 500000.0
BIG = 30000.0
NFM = 17
NCOL = NFM * 128 + 512
QKROWS = 1024


class Buf:
    __slots__ = ("ap", "lw", "rd", "name")

    def __init__(self, ap, name=""):
        self.ap = ap
        self.lw = {}
        self.rd = {}
        self.name = name

    def __getitem__(self, idx):
        return self.ap[idx]


class Engine:
    def __init__(self, fw, eng, name):
        self.fw = fw
        self.eng = eng
        self.name = name
        self.sem = fw.nc.alloc_semaphore(name="s_" + name)
        self.cnt = 0
        self.seen = {}

    def _wait(self, tok):
        sem, val, _ = tok
        k = id(sem)
        if self.seen.get(k, 0) >= val:
            return
        self.seen[k] = val
        self.eng.wait_ge(sem, val)

    def deps(self, reads, writes, acc=False):
        for b in reads:
            for t in b.lw.values():
                self._wait(t)
        for b in writes:
            for t in b.lw.values():
                if acc and t[2] == self.name:
                    continue
                self._wait(t)
            for t in b.rd.values():
                self._wait(t)

    def done(self, ins, reads, writes):
        self.cnt += 1
        ins.then_inc(self.sem, 1)
        tok = (self.sem, self.cnt, self.name)
        k = id(self.sem)
        for b in reads:
            b.rd[k] = tok
        for b in writes:
            b.lw[k] = tok
            b.rd = {}
        return tok

    def op(self, fn, reads=(), writes=(), acc=False):
        self.deps(reads, writes, acc)
        ins = fn()
        return self.done(ins, reads, writes)


class DmaQueue:
    def __init__(self, fw, engine, nsem, name):
        self.engine = engine
        self.sems = [fw.nc.alloc_semaphore(name=f"d_{name}{i}") for i in range(nsem)]
        self.cnts = [0] * nsem
        self.rr = 0
        self.name = name

    def dma(self, out_ap, in_ap, reads=(), writes=()):
        E = self.engine
        j = self.rr
        self.rr = (self.rr + 1) % len(self.sems)
        sem = self.sems[j]
        if self.cnts[j] > 0:
            E._wait((sem, 16 * self.cnts[j], "dma"))
        for b in reads:
            for t in b.lw.values():
                E._wait(t)
        for b in writes:
            for t in b.lw.values():
                E._wait(t)
            for t in b.rd.values():
                E._wait(t)
        ins = E.eng.dma_start(out=out_ap, in_=in_ap)
        self.cnts[j] += 1
        ins.then_inc(sem, 16)
        tok = (sem, 16 * self.cnts[j], "dma_" + self.name + str(j))
        k = id(sem)
        for b in reads:
            b.rd[k] = tok
        for b in writes:
            b.lw[k] = tok
            b.rd = {}
        return tok

    def all_tokens(self):
        return [(s, 16 * c, "dma") for s, c in zip(self.sems, self.cnts) if c > 0]


class FW:
    def __init__(self, nc):
        self.nc = nc
        self.pe = Engine(self, nc.tensor, "pe")
        self.act = Engine(self, nc.scalar, "act")
        self.dve = Engine(self, nc.vector, "dve")
        self.pool = Engine(self, nc.gpsimd, "pool")
        self.sp = Engine(self, nc.sync, "sp")
        self.engines = [self.pe, self.act, self.dve, self.pool, self.sp]
        self.q_ld = DmaQueue(self, self.sp, 12, "sp")
        self.q_st = DmaQueue(self, self.pool, 8, "pl")
        self._ctx = []
        self._n = 0

    def sbuf(self, shape, dtype, name=None):
        self._n += 1
        nm = (name or "sb") + str(self._n)
        g = self.nc.sbuf_tensor(nm, list(shape), dtype)
        t = g.__enter__()
        self._ctx.append(g)
        return Buf(t, nm)

    def psum(self, shape, dtype, name=None):
        self._n += 1
        nm = (name or "ps") + str(self._n)
        g = self.nc.psum_tensor(nm, list(shape), dtype)
        t = g.__enter__()
        self._ctx.append(g)
        return Buf(t, nm)

    def mark(self):
        return len(self._ctx)

    def release(self, mark):
        while len(self._ctx) > mark:
            self._ctx.pop().__exit__(None, None, None)

    def barrier(self):
        toks = [(e.sem, e.cnt, e.name) for e in self.engines if e.cnt > 0]
        toks += self.q_ld.all_tokens() + self.q_st.all_tokens()
        for e in self.engines:
            for t in toks:
                if t[0] is not e.sem:
                    e._wait(t)
        for e in self.engines:
            if e.cnt > 4000:
                e.sem = self.nc.alloc_semaphore(name="s_%s_%d" % (e.name, self._n))
                self._n += 1
                e.cnt = 0

    def finish(self):
        for t in self.q_ld.all_tokens() + self.q_st.all_tokens():
            self.sp._wait(t)


class Ring:
    def __init__(self, bufs):
        self.bufs = bufs
        self.i = 0

    def get(self):
        b = self.bufs[self.i]
        self.i = (self.i + 1) % len(self.bufs)
        return b


class K:
    def __init__(self, nc, S):
        self.nc = nc
        self.S = S
        self.fw = FW(nc)
        self.v = nc.vector
        self.a = nc.scalar
        self.g = nc.gpsimd
        self.t = nc.tensor

    def mm(self, ob, oap, lb, lap, rb, rap, start, stop):
        nc = self.nc
        self.fw.pe.op(lambda: nc.tensor.matmul(oap, lap, rap, start=start, stop=stop),
                      [lb, rb], [ob], acc=not start)

    def tr(self, ob, oap, ib, iap, identb):
        nc = self.nc
        self.fw.pe.op(lambda: nc.tensor.transpose(oap, iap, identb.ap[:]), [ib, identb], [ob], acc=True)

    def act(self, ob, oap, ib, iap, func, extra_reads=(), **kw):
        nc = self.nc
        self.fw.act.op(lambda: nc.scalar.activation(oap, iap, func, **kw), [ib] + list(extra_reads), [ob])

    def ld(self, ob, oap, src_ap, srcbuf=None):
        return self.fw.q_ld.dma(oap, src_ap, reads=[srcbuf] if srcbuf is not None else [], writes=[ob])

    def st(self, dst_ap, ib, iap, dstbuf=None):
        return self.fw.q_st.dma(dst_ap, iap, reads=[ib], writes=[dstbuf] if dstbuf is not None else [])

    def dve(self, fn, reads, writes):
        self.fw.dve.op(fn, reads, writes)

    def pool(self, fn, reads, writes):
        self.fw.pool.op(fn, reads, writes)


def load_consts(k, cin):
    fw, nc = k.fw, k.nc
    c = {}
    for name, shape, dt in [("ident", [128, 128], BF16), ("maskle", [128, 128], BF16),
                            ("masklt", [128, 128], BF16), ("maskltp", [128, 128], BF16),
                            ("tricum", [128, 128], BF16), ("ones", [128, 128], BF16), ("zeros", [128, 128], BF16),
                            ("onehot", [32, 32 * 128], BF16), ("sel65", [65, 64], F32),
                            ("ones64", [64, 64], F32), ("ropec", [128, 4], F32)]:
        b = fw.sbuf(shape, dt, "c_" + name)
        k.ld(b, b[:], cin[name][:, :])
        c[name] = b
    c["_onehot_dram"] = cin["onehot"]
    return c


def rms_stats(k, xt, ntile, ss, rstd, junk):
    nc = k.nc
    for t in range(ntile):
        k.fw.act.op(lambda t=t: nc.scalar.activation(junk[:], xt[:, t, :], AF.Square, accum_out=ss[:, t:t + 1]),
                    [xt], [junk, ss])
    k.fw.act.op(lambda: nc.scalar.activation(rstd[:, 0:ntile], ss[:, 0:ntile], AF.Sqrt, bias=EPS, scale=1.0 / D),
                [ss], [rstd])
    k.dve(lambda: nc.vector.reciprocal(rstd[:, 0:ntile], rstd[:, 0:ntile]), [rstd], [rstd])


def norm_transpose(k, c, xt, ntile, rstd, gcol, xs, hnT, psT_ring):
    nc = k.nc
    for t in range(ntile):
        k.dve(lambda t=t: nc.vector.tensor_scalar(xs[:, t, :], xt[:, t, :], rstd[:, t:t + 1], None, op0=ALU.mult),
              [xt, rstd], [xs])
    for ch in range(8):
        ps = psT_ring.get()
        for t in range(ntile):
            k.tr(ps, ps[:, t * 128:(t + 1) * 128], xs, xs[:, t, ch * 128:(ch + 1) * 128], c["ident"])
        if ch % 2 == 0:
            k.dve(lambda ch=ch, ps=ps: nc.vector.tensor_scalar(hnT[:, ch, 0:ntile * 128], ps[:, 0:ntile * 128],
                                                               gcol[:, ch:ch + 1], None, op0=ALU.mult),
                  [ps, gcol], [hnT])
        else:
            k.fw.act.op(lambda ch=ch, ps=ps: nc.scalar.activation(hnT[:, ch, 0:ntile * 128], ps[:, 0:ntile * 128],
                                                                  AF.Copy, scale=gcol[:, ch:ch + 1]),
                        [ps, gcol], [hnT])


def load_weights_bf16(k, wdram, ncols, wsb, stage_ring, col_chunk=512):
    nc = k.nc
    i = 0
    for ch in range(8):
        for c0 in range(0, ncols, col_chunk):
            cw = min(col_chunk, ncols - c0)
            stg = stage_ring.get()
            k.ld(stg, stg[:, 0:cw], wdram[ch * 128:(ch + 1) * 128, c0:c0 + cw])
            if i % 2 == 0:
                k.pool(lambda stg=stg, ch=ch, c0=c0, cw=cw: nc.gpsimd.tensor_copy(wsb[:, ch, c0:c0 + cw], stg[:, 0:cw]),
                       [stg], [wsb])
            else:
                k.dve(lambda stg=stg, ch=ch, c0=c0, cw=cw: nc.vector.tensor_copy(wsb[:, ch, c0:c0 + cw], stg[:, 0:cw]),
                      [stg], [wsb])
            i += 1


def build_rope_tables(k, c, pos_dram, rope_dram, rope_buf):
    nc, S = k.nc, k.S
    fw = k.fw
    mark = fw.mark()
    CH = min(S, 2048)
    pi_ = fw.sbuf([128, CH], I32, "posi")
    pf = fw.sbuf([128, CH], F32, "posf")
    ang = fw.sbuf([128, CH], F32, "ang")
    t2 = fw.sbuf([128, CH], F32, "t2")
    o = Ring([fw.sbuf([128, CH], F32, "ro") for _ in range(2)])
    MAG = 12582912.0
    for c0 in range(0, S, CH):
        src = pos_dram[0:1, c0:c0 + CH]
        k.ld(pi_, pi_[:], src.partition_broadcast(128))
        k.act(pf, pf[:], pi_, pi_[:], AF.Copy)
        for pat in range(2):
            inv = c["ropec"][:, 2 * pat:2 * pat + 1]
            sgn = c["ropec"][:, 2 * pat + 1:2 * pat + 2]
            for which in range(2):
                k.dve(lambda inv=inv, which=which: nc.vector.tensor_scalar(
                    ang[:], pf[:], inv, (math.pi / 2 if which == 0 else 0.0), op0=ALU.mult, op1=ALU.add),
                    [pf, c["ropec"]], [ang])
                k.dve(lambda: nc.vector.tensor_scalar(t2[:], ang[:], 1.0 / (2 * math.pi), MAG, op0=ALU.mult, op1=ALU.add),
                      [ang], [t2])
                k.dve(lambda: nc.vector.tensor_scalar(t2[:], t2[:], MAG, -2 * math.pi, op0=ALU.subtract, op1=ALU.mult),
                      [t2], [t2])
                k.dve(lambda: nc.vector.tensor_tensor(ang[:], ang[:], t2[:], op=ALU.add), [ang, t2], [ang])
                k.dve(lambda: nc.vector.tensor_scalar(ang[:], ang[:], -3.141592, 3.141592, op0=ALU.max, op1=ALU.min),
                      [ang], [ang])
                ob = o.get()
                if which == 0:
                    k.act(ob, ob[:], ang, ang[:], AF.Sin)
                else:
                    k.act(t2, t2[:], ang, ang[:], AF.Sin)
                    k.dve(lambda ob=ob, sgn=sgn: nc.vector.tensor_scalar(ob[:], t2[:], sgn, None, op0=ALU.mult),
                          [t2, c["ropec"]], [ob])
                k.st(rope_dram[2 * pat + which, :, c0:c0 + CH], ob, ob[:], rope_buf)
    fw.barrier()
    fw.release(mark)


def stage_a_proj(k, c, hin, win, gin, rope_dram, rope_buf, qk_dram, qk_buf, gt_dram, gt_buf, v_dram, v_buf):
    nc, S, fw = k.nc, k.S, k.fw
    mark = fw.mark()
    NGRP = S // 512
    wsb = fw.sbuf([128, 8, NCOL], BF16, "wsb")
    stage_ring = Ring([fw.sbuf([128, 512], F32, "wstg") for _ in range(3)])
    gcol = fw.sbuf([128, 8], F32, "gcol")
    k.ld(gcol, gcol[:], gin[:, :])
    load_weights_bf16(k, win, NCOL, wsb, stage_ring)
    xt_ring = Ring([fw.sbuf([128, 4, 1024], F32, "xt") for _ in range(2)])
    xs = fw.sbuf([128, 4, 1024], BF16, "xs")
    hnT_ring = Ring([fw.sbuf([128, 8, 512], BF16, "hnT") for _ in range(2)])
    junk = fw.sbuf([128, 1024], BF16, "junk")
    ss = fw.sbuf([128, 8], F32, "ss")
    rstd = fw.sbuf([128, 8], F32, "rstd")
    rope_ring = Ring([fw.sbuf([128, 4, 512], F32, "rope") for _ in range(2)])
    tmp_ring = Ring([fw.sbuf([128, 512], F32, "ptmp") for _ in range(4)])
    ob_ring = Ring([fw.sbuf([128, 512], BF16, "pob") for _ in range(6)])
    vo_ring = Ring([fw.sbuf([128, 512], BF16, "vob") for _ in range(3)])
    psT = Ring([fw.psum([128, 512], BF16, "psT") for _ in range(2)])
    psP = Ring([fw.psum([128, 512], F32, "psP") for _ in range(5)])

    fm = [(0, 3, 0, 0), (1, 4, 0, 128), (2, 5, 0, 256),
          (6, 8, 1, 384), (7, 9, 1, 512),
          (10, None, 2, 640), (11, None, 2, 768), (12, None, 2, 896)]
    for g in range(NGRP):
        t0 = g * 512
        xt = xt_ring.get()
        k.ld(xt, xt[:], hin[t0:t0 + 512, :].rearrange("(t p) d -> p t d", p=128))
        rp = rope_ring.get()
        k.ld(rp, rp[:], rope_dram[:, :, t0:t0 + 512].rearrange("f p s -> p f s"), rope_buf)
        rms_stats(k, xt, 4, ss, rstd, junk)
        hnT = hnT_ring.get()
        norm_transpose(k, c, xt, 4, rstd, gcol, xs, hnT, psT)

        def proj(cg):
            ps = psP.get()
            for ch in range(8):
                k.mm(ps, ps[:, :], wsb, wsb[:, ch, cg * 128:(cg + 1) * 128], hnT, hnT[:, ch, :], ch == 0, ch == 7)
            return ps

        for (cg, pg, kind, row0) in fm:
            ps = proj(cg)
            ob = ob_ring.get()
            if pg is None:
                if cg % 2 == 0:
                    k.dve(lambda ob=ob, ps=ps: nc.vector.tensor_copy(ob[:], ps[:]), [ps], [ob])
                else:
                    k.act(ob, ob[:], ps, ps[:], AF.Copy)
            else:
                ps2 = proj(pg)
                t1 = tmp_ring.get()
                t2 = tmp_ring.get()
                k.dve(lambda t1=t1, ps=ps, kind=kind: nc.vector.tensor_tensor(t1[:], ps[:], rp[:, 2 * kind, :], op=ALU.mult),
                      [ps, rp], [t1])
                k.dve(lambda t2=t2, ps2=ps2, kind=kind: nc.vector.tensor_tensor(t2[:], ps2[:], rp[:, 2 * kind + 1, :], op=ALU.mult),
                      [ps2, rp], [t2])
                k.pool(lambda ob=ob, t1=t1, t2=t2: nc.gpsimd.tensor_tensor(ob[:], t1[:], t2[:], op=ALU.add), [t1, t2], [ob])
            k.st(qk_dram[row0:row0 + 128, t0:t0 + 512], ob, ob[:], qk_buf)
        for gi in range(4):
            ps = proj(13 + gi)
            ob = ob_ring.get()
            k.act(ob, ob[:], ps, ps[:], AF.Silu)
            k.st(gt_dram[gi * 128:(gi + 1) * 128, t0:t0 + 512], ob, ob[:], gt_buf)
        for t in range(4):
            ps = psP.get()
            for ch in range(8):
                k.mm(ps, ps[:, :], hnT, hnT[:, ch, t * 128:(t + 1) * 128], wsb, wsb[:, ch, NFM * 128:NFM * 128 + 512],
                     ch == 0, ch == 7)
            vo = vo_ring.get()
            k.dve(lambda vo=vo, ps=ps: nc.vector.tensor_copy(vo[:], ps[:]), [ps], [vo])
            k.st(v_dram[t0 + t * 128:t0 + (t + 1) * 128, :], vo, vo[:], v_buf)
    fw.barrier()
    fw.release(mark)


def stage_a_attn(k, c, lamin, hng, lam_init, qk_dram, qk_buf, gt_dram, gt_buf, v_dram, v_buf, mt_dram, mt_buf, mrow0=0):
    nc, S, fw = k.nc, k.S, k.fw
    mark = fw.mark()
    NG = S // 512
    NT = S // 128
    NB = S // 256
    ident, ones = c["ident"], c["ones"]
    cin_onehot = c["_onehot_dram"]

    KT = fw.sbuf([128, S], BF16, "KT")
    KN = fw.sbuf([128, S], BF16, "KN")
    QT = fw.sbuf([128, S], BF16, "QT")
    K2 = fw.sbuf([128, S], BF16, "K2")
    Vt = fw.sbuf([128, NT, 128], BF16, "Vt")
    MT = fw.sbuf([32, S], BF16, "MT")
    E_ring = Ring([fw.sbuf([128, 512], BF16, "E") for _ in range(4)])
    E32_ring = Ring([fw.sbuf([128, 512], F32, "E32") for _ in range(2)])
    SP_ring = Ring([fw.sbuf([128, 512], BF16, "SP") for _ in range(3)])
    R = fw.sbuf([128, 512], BF16, "R")
    ob32 = Ring([fw.sbuf([65, 512], F32, "o32") for _ in range(3)])
    rc = Ring([fw.sbuf([64, 512], F32, "rc") for _ in range(3)])
    gtile = Ring([fw.sbuf([64, 512], BF16, "gt") for _ in range(2)])
    mout = Ring([fw.sbuf([64, 512], BF16, "mo") for _ in range(2)])
    kmf = fw.sbuf([64, 32], F32, "kmf")
    kmb = fw.sbuf([64, 32], BF16, "kmb")
    G32 = fw.sbuf([128, 32], F32, "G32")
    m8 = fw.sbuf([128, 8], F32, "m8")
    Mb = fw.sbuf([128, 96], BF16, "Mb")
    lamv = fw.sbuf([64, 128], F32, "lamv")
    lamj = fw.sbuf([64, 32], F32, "lamj")
    lam2 = fw.sbuf([64, 4], F32, "lam2")
    lamc = fw.sbuf([64, 1], F32, "lamc")
    gsc = fw.sbuf([64, 1], F32, "gsc")
    psW = Ring([fw.psum([128, 512], F32, "psW") for _ in range(4)])
    psO1 = fw.psum([128, 512], F32, "psO1")
    psO2 = fw.psum([128, 512], F32, "psO2")
    psX = fw.psum([128, 512], F32, "psX")
    psXb = fw.psum([128, 512], BF16, "psXb")

    E_zero = fw.sbuf([128, 512], BF16, "Ezero")
    k.pool(lambda: nc.gpsimd.memset(E_zero[:], 0.0), [], [E_zero])
    k.pool(lambda: nc.gpsimd.memset(Vt[:, :, 64:128], 1.0), [], [Vt])

    k.ld(lamv, lamv[:], lamin[0:1, :].partition_broadcast(64))
    k.ld(gsc, gsc[:], hng[:, :])
    for i in range(2):
        k.dve(lambda i=i: nc.vector.tensor_tensor(lamj[:], lamv[:, 64 * i:64 * i + 32], lamv[:, 64 * i + 32:64 * i + 64], op=ALU.mult),
              [lamv], [lamj])
        k.dve(lambda i=i: nc.vector.reduce_sum(lam2[:, i:i + 1], lamj[:], axis=AX.X), [lamj], [lam2])
    k.act(lam2, lam2[:, 2:4], lam2, lam2[:, 0:2], AF.Exp)
    k.dve(lambda: nc.vector.tensor_tensor(lamc[:], lam2[:, 3:4], lam2[:, 2:3], op=ALU.subtract), [lam2], [lamc])
    k.dve(lambda: nc.vector.tensor_scalar_add(lamc[:], lamc[:], -lam_init), [lamc], [lamc])
    k.dve(lambda: nc.vector.tensor_scalar_mul(gsc[:], gsc[:], 1.0 - lam_init), [gsc], [gsc])

    def load_head(qrow, krow, vcol, need_q):
        for s0 in range(0, S, 2048):
            s1 = min(S, s0 + 2048)
            k.ld(KT, KT[0:64, s0:s1], qk_dram[krow:krow + 64, s0:s1], qk_buf)
            if need_q:
                k.ld(QT, QT[0:64, s0:s1], qk_dram[qrow:qrow + 64, s0:s1], qk_buf)
        for t0 in range(0, NT, 16):
            t1 = min(NT, t0 + 16)
            k.ld(Vt, Vt[:, t0:t1, 0:64],
                 v_dram[t0 * 128:t1 * 128, vcol:vcol + 64].rearrange("(t p) d -> p t d", p=128), v_buf)

    def finish_group(kind, g, grow, mrow, o_list):
        t0 = g * 512
        gt_ = gtile.get()
        k.ld(gt_, gt_[:], gt_dram[grow:grow + 64, t0:t0 + 512], gt_buf)
        mo = mout.get()
        if kind == "sb":
            k.dve(lambda: nc.vector.tensor_tensor(mo[:], o_list[0][0:64, :], gt_[:], op=ALU.mult), [o_list[0], gt_], [mo])
        else:
            res = []
            for po in o_list:
                o32 = ob32.get()
                k.dve(lambda o32=o32, po=po: nc.vector.tensor_copy(o32[:], po[0:65, :]), [po], [o32])
                k.mm(psX, psX[0:64, :], c["sel65"], c["sel65"][:, :], o32, o32[:, :], True, True)
                r = rc.get()
                k.dve(lambda r=r: nc.vector.reciprocal(r[:], psX[0:64, :]), [psX], [r])
                k.dve(lambda r=r, o32=o32: nc.vector.tensor_tensor(r[:], r[:], o32[0:64, :], op=ALU.mult), [r, o32], [r])
                res.append(r)
            if kind == "moba":
                k.dve(lambda: nc.vector.tensor_tensor(mo[:], res[0][:], gt_[:], op=ALU.mult), [res[0], gt_], [mo])
            else:
                a = res[0]
                k.dve(lambda: nc.vector.scalar_tensor_tensor(a[:], res[1][:], lamc[:, 0:1], a[:], op0=ALU.mult, op1=ALU.add),
                      [res[1], a, lamc], [a])
                sq = res[1]
                k.dve(lambda: nc.vector.tensor_tensor(sq[:], a[:], a[:], op=ALU.mult), [a], [sq])
                k.mm(psX, psX[0:64, :], c["ones64"], c["ones64"][:, :], sq, sq[:, :], True, True)
                k.act(sq, sq[:], psX, psX[0:64, :], AF.Ln, bias=EPS)
                k.act(sq, sq[:], sq, sq[:], AF.Exp, scale=-0.5)
                k.dve(lambda: nc.vector.scalar_tensor_tensor(a[:], a[:], gsc[:, 0:1], sq[:], op0=ALU.mult, op1=ALU.mult),
                      [a, gsc, sq], [a])
                k.dve(lambda: nc.vector.tensor_tensor(mo[:], a[:], gt_[:], op=ALU.mult), [a, gt_], [mo])
        k.st(mt_dram[mrow0 + mrow:mrow0 + mrow + 64, t0:t0 + 512], mo, mo[:], mt_buf)

    def zinit(po):
        k.mm(po, po[:, :], c["zeros"], c["zeros"][:, :], E_zero, E_zero[:, :], True, False)

    def pv(po, kb, g, Eb, sub_first_is_start, descending):
        j = kb - 4 * g
        if j < 0:
            last = (kb == 0) and descending
            k.mm(po, po[:, :], Vt, Vt[:, kb, :], Eb, Eb[:, :], False, last)
        else:
            for s in range(j, 4):
                stop = (kb == 0) if descending else (s == j)
                k.mm(po, po[:, s * 128:(s + 1) * 128], Vt, Vt[:, kb, :], Eb, Eb[:, s * 128:(s + 1) * 128], False, stop)

    def run_pipeline(items, stages):
        n = len(items)
        mx = max(sk for sk, _ in stages)
        for step in range(n + mx):
            for sk, fn in stages:
                i = step - sk
                if 0 <= i < n:
                    fn(items[i])

    Rb = [R, fw.sbuf([128, 512], BF16, "R2")]

    sc64 = 64 ** -0.5
    for h in range(3):
        qrow = [0, 64, 256][h]
        krow = [128, 192, 320][h]
        load_head(qrow, krow, 64 * h, True)
        k.dve(lambda: nc.vector.tensor_reduce(kmf[:, 0:NB], KT[0:64, :].rearrange("p (n l) -> p n l", l=256), axis=AX.X, op=ALU.add),
              [KT], [kmf])
        k.dve(lambda: nc.vector.tensor_scalar_mul(kmb[:, 0:NB], kmf[:, 0:NB], 1.0 / 256), [kmf], [kmb])
        if h == 0:
            ohv = cin_onehot.rearrange("p (n i) -> p n i", i=128)
            for two in range(2):
                k.ld(KT, KT[64:96, :].rearrange("p (n two i) -> p n two i", two=2, i=128)[:, :, two, :], ohv[:, 0:NB, :])
            k.pool(lambda: nc.gpsimd.memset(Mb[:, 0:64], 0.0), [], [Mb])
        for qt in range(NT):
            own = qt // 2
            k.pool(lambda: nc.gpsimd.memset(Mb[:, 64:96], -BIG), [], [Mb])
            k.pool(lambda own=own: nc.gpsimd.memset(Mb[:, 64 + own:65 + own], 0.0), [], [Mb])
            if own > 0:
                k.mm(psX, psX[:, 0:own], QT, QT[0:64, qt * 128:(qt + 1) * 128], kmb, kmb[:, 0:own], True, True)
                k.dve(lambda: nc.vector.memset(G32[:], -1e30), [], [G32])
                k.dve(lambda own=own: nc.vector.tensor_copy(G32[:, 0:own], psX[:, 0:own]), [psX], [G32])
                k.dve(lambda: nc.vector.max(m8[:], G32[:]), [G32], [m8])
                k.dve(lambda own=own: nc.vector.tensor_scalar(Mb[:, 64:64 + own], G32[:, 0:own], m8[:, 2:3], -BIG,
                                                              op0=ALU.is_lt, op1=ALU.mult), [G32, m8], [Mb])
            k.tr(psXb, psXb[0:96, 0:128], Mb, Mb[:, :], ident)
            k.dve(lambda qt=qt: nc.vector.tensor_copy(QT[64:96, qt * 128:(qt + 1) * 128], psXb[64:96, 0:128]), [psXb], [QT])
        items = []
        for g in range(NG):
            for kb in range(4 * g + 4):
                j = kb - 4 * g
                items.append(dict(g=g, kb=kb, j=j, q0=(128 * j if j > 0 else 0), first=(kb == 0), last=(kb == 4 * g + 3),
                                  po=(psO1 if g % 2 == 0 else psO2)))

        def m_s(it):
            g, kb, j, q0 = it["g"], it["kb"], it["j"], it["q0"]
            n = kb // 2
            ps = psW.get()
            it["ps"] = ps
            k.mm(ps, ps[:, q0:512], KT, KT[0:96, kb * 128:(kb + 1) * 128], QT, QT[0:96, g * 512 + q0:(g + 1) * 512], True, j < 0)
            if j >= 0:
                k.mm(ps, ps[:, q0:q0 + 128], ident, ident[:, :], c["maskle"], c["maskle"][:, :], False, True)

        def m_e(it):
            q0, ps = it["q0"], it["ps"]
            Eb = E_ring.get()
            it["E"] = Eb
            k.act(Eb, Eb[:, q0:512], ps, ps[:, q0:512], AF.Exp, scale=sc64)

        def m_pv(it, h=h):
            if it["first"]:
                zinit(it["po"])
            pv(it["po"], it["kb"], it["g"], it["E"], True, False)
            if it["last"]:
                finish_group("moba", it["g"], 64 * h, 64 * h, [it["po"]])

        run_pipeline(items, [(0, m_s), (0, m_e), (2, m_pv)])

    sc32 = 32 ** -0.5
    for h in range(2):
        qrow = 384 + 64 * h
        krow = 512 + 64 * h
        for s0 in range(0, S, 2048):
            s1 = min(S, s0 + 2048)
            for r in range(4):
                k.ld(KT, KT[32 * r:32 * r + 32, s0:s1], qk_dram[krow:krow + 32, s0:s1], qk_buf)
                k.ld(K2, K2[32 * r:32 * r + 32, s0:s1], qk_dram[krow + 32:krow + 64, s0:s1], qk_buf)
                k.ld(QT, QT[32 * r:32 * r + 32, s0:s1], qk_dram[qrow:qrow + 32, s0:s1], qk_buf)
                k.ld(KN, KN[32 * r:32 * r + 32, s0:s1], qk_dram[qrow + 32:qrow + 64, s0:s1], qk_buf)
        for t0_ in range(0, NT, 16):
            t1_ = min(NT, t0_ + 16)
            k.ld(Vt, Vt[:, t0_:t1_, 0:64],
                 v_dram[t0_ * 128:t1_ * 128, 192 + 64 * h:256 + 64 * h].rearrange("(t p) d -> p t d", p=128), v_buf)
        items = []
        for g in range(NG):
            for kb in range(4 * g + 4):
                j = kb - 4 * g
                for cc in range(2):
                    items.append(dict(g=g, kb=kb, j=j, q0=(128 * j if j > 0 else 0), cc=cc, first=(kb == 0 and cc == 0),
                                      last=(kb == 4 * g + 3 and cc == 1)))

        def d_s(it):
            g, kb, j, q0, cc = it["g"], it["kb"], it["j"], it["q0"], it["cc"]
            ps = psW.get()
            it["ps"] = ps
            Qz = QT if cc == 0 else KN
            Kz = KT if cc == 0 else K2
            k.mm(ps, ps[:, q0:512], Kz, Kz[:, kb * 128:(kb + 1) * 128], Qz, Qz[:, g * 512 + q0:(g + 1) * 512], True, j < 0)
            if j >= 0:
                k.mm(ps, ps[:, q0:q0 + 128], ident, ident[:, :], c["maskle"], c["maskle"][:, :], False, True)

        def d_e(it):
            q0, ps = it["q0"], it["ps"]
            Eb = E_ring.get()
            it["E"] = Eb
            k.act(Eb, Eb[:, q0:512], ps, ps[:, q0:512], AF.Exp, scale=sc32 / 4.0)

        def d_pv(it, h=h):
            if it["first"]:
                zinit(psO1)
                zinit(psO2)
            pv(psO1 if it["cc"] == 0 else psO2, it["kb"], it["g"], it["E"], True, False)
            if it["last"]:
                finish_group("diff", it["g"], 192 + 64 * h, 192 + 64 * h, [psO1, psO2])

        run_pipeline(items, [(0, d_s), (0, d_e), (2, d_pv)])

    for h in range(3):
        qrow = [640, 704, 896][h]
        krow = [768, 832, 960][h]
        load_head(qrow, krow, 320 + 64 * h, True)
        k.dve(lambda: nc.vector.tensor_scalar(KN[0:64, :], KT[0:64, :], -sc64, None, op0=ALU.mult), [KT], [KN])
        items = []
        for g in range(NG):
            nkb = 4 * g + 4
            for idx, kb in enumerate(range(nkb - 1, -1, -1)):
                j = kb - 4 * g
                items.append(dict(g=g, kb=kb, j=j, q0=(128 * j if j > 0 else 0), idx=idx, first=(idx == 0), last=(kb == 0),
                                  po=(psO1 if g % 2 == 0 else psO2)))

        def s_z(it):
            g, kb, j, q0 = it["g"], it["kb"], it["j"], it["q0"]
            qsl = slice(g * 512 + q0, (g + 1) * 512)
            pz = psW.get()
            it["pz"] = pz
            k.mm(pz, pz[:, q0:512], KT, KT[0:64, kb * 128:(kb + 1) * 128], QT, QT[0:64, qsl], True, j < 0)
            if j >= 0:
                k.mm(pz, pz[:, q0:q0 + 128], ident, ident[:, :], c["masklt"], c["masklt"][:, :], False, True)

        def s_sp(it):
            q0, pz = it["q0"], it["pz"]
            E32 = E32_ring.get()
            k.act(E32, E32[:, q0:512], pz, pz[:, q0:512], AF.Exp, scale=sc64)
            SPb = SP_ring.get()
            it["SP"] = SPb
            k.act(SPb, SPb[:, q0:512], E32, E32[:, q0:512], AF.Ln, bias=1.0)

        def s_u(it):
            g, kb, j, q0, idx, SPb = it["g"], it["kb"], it["j"], it["q0"], it["idx"], it["SP"]
            qsl = slice(g * 512 + q0, (g + 1) * 512)
            Rp = Rb[(idx + 1) % 2]
            Rn = Rb[idx % 2]
            pu = psW.get()
            it["pu"] = pu
            k.mm(pu, pu[:, q0:512], KN, KN[0:64, kb * 128:(kb + 1) * 128], QT, QT[0:64, qsl], True, False)
            lo = q0 + 128 if j >= 0 else 0
            if idx > 0 and lo < 512:
                k.mm(pu, pu[:, lo:512], ones, ones[:, :], Rp, Rp[:, lo:512], False, False)
            if j >= 0:
                k.mm(pu, pu[:, q0:q0 + 128], ident, ident[:, :], c["maskltp"], c["maskltp"][:, :], False, False)
            k.mm(pu, pu[:, q0:512], c["tricum"], c["tricum"][:, :], SPb, SPb[:, q0:512], False, True)
            if kb > 0:
                if j >= 0:
                    k.dve(lambda: nc.vector.tensor_copy(Rn[:, q0:q0 + 128], SPb[:, q0:q0 + 128]), [SPb], [Rn])
                    if lo < 512:
                        if idx > 0:
                            k.dve(lambda: nc.vector.tensor_tensor(Rn[:, lo:512], Rp[:, lo:512], SPb[:, lo:512], op=ALU.add),
                                  [SPb, Rp], [Rn])
                        else:
                            k.dve(lambda: nc.vector.tensor_copy(Rn[:, lo:512], SPb[:, lo:512]), [SPb], [Rn])
                else:
                    k.dve(lambda: nc.vector.tensor_tensor(Rn[:, :], Rp[:, :], SPb[:, :], op=ALU.add), [SPb, Rp], [Rn])

        def s_a(it):
            q0, pu = it["q0"], it["pu"]
            Ab = E_ring.get()
            it["A"] = Ab
            k.act(Ab, Ab[:, q0:512], pu, pu[:, q0:512], AF.Exp, scale=-1.0)

        def s_pv(it, h=h):
            if it["first"]:
                zinit(it["po"])
            pv(it["po"], it["kb"], it["g"], it["A"], True, True)
            if it["last"]:
                finish_group("sb", it["g"], 320 + 64 * h, 320 + 64 * h, [it["po"]])

        run_pipeline(items, [(0, s_z), (0, s_sp), (1, s_u), (1, s_a), (2, s_pv)])
    fw.barrier()
    fw.release(mark)


def stage_b(k, c, ST, hin, mt_full, wout, gx, memin, gm, wxq, wxkv, wxo, gfin, hout, final):
    nc, fw = k.nc, k.fw
    mark = fw.mark()
    ident = c["ident"]
    NGRP = ST // 512
    stage_ring = Ring([fw.sbuf([128, 512], F32, "wstg") for _ in range(3)])
    wo = fw.sbuf([128, 8, 1024], BF16, "wo")
    wq = fw.sbuf([128, 8, 512], BF16, "wq")
    wkv = fw.sbuf([128, 8, 1024], BF16, "wkv")
    wxo_s = fw.sbuf([128, 4, 1024], BF16, "wxo")
    load_weights_bf16(k, wout, 1024, wo, stage_ring)
    load_weights_bf16(k, wxq, 512, wq, stage_ring)
    load_weights_bf16(k, wxkv, 1024, wkv, stage_ring)
    for ch in range(4):
        for c0 in (0, 512):
            stg = stage_ring.get()
            k.ld(stg, stg[:, :], wxo[ch * 128:(ch + 1) * 128, c0:c0 + 512])
            k.dve(lambda stg=stg, ch=ch, c0=c0: nc.vector.tensor_copy(wxo_s[:, ch, c0:c0 + 512], stg[:, :]), [stg], [wxo_s])
    gxc = fw.sbuf([128, 8], F32, "gxc")
    gmc = fw.sbuf([128, 8], F32, "gmc")
    k.ld(gxc, gxc[:], gx[:, :])
    k.ld(gmc, gmc[:], gm[:, :])
    gf = fw.sbuf([128, 1024], F32, "gf")
    if final:
        k.ld(gf, gf[:], gfin[0:1, :].partition_broadcast(128))
    junk = fw.sbuf([128, 1024], BF16, "junk")
    ss = fw.sbuf([128, 8], F32, "ss")
    rstd = fw.sbuf([128, 8], F32, "rstd")
    xs = fw.sbuf([128, 4, 1024], BF16, "xs")
    psT = Ring([fw.psum([128, 512], BF16, "psT") for _ in range(2)])
    psP = Ring([fw.psum([128, 512], F32, "psP") for _ in range(4)])
    psA = fw.psum([128, 512], F32, "psA")
    psL = fw.psum([128, 512], F32, "psL")

    xt_ring = Ring([fw.sbuf([128, 4, 1024], F32, "xt") for _ in range(2)])
    memt = xt_ring.bufs[0]
    k.ld(memt, memt[:, 0:2, :], memin[:, :].rearrange("(t p) d -> p t d", p=128))
    rms_stats(k, memt, 2, ss, rstd, junk)
    mnT = fw.sbuf([128, 8, 512], BF16, "mnT")
    norm_transpose(k, c, memt, 2, rstd, gmc, xs, mnT, psT)
    KxT = fw.sbuf([128, 4, 256], BF16, "KxT")
    Vx = fw.sbuf([128, 2, 512], BF16, "Vx")
    for hh in range(4):
        ps = psP.get()
        for ch in range(8):
            k.mm(ps, ps[:, 0:256], wkv, wkv[:, ch, hh * 128:(hh + 1) * 128], mnT, mnT[:, ch, 0:256], ch == 0, ch == 7)
        k.dve(lambda hh=hh, ps=ps: nc.vector.tensor_copy(KxT[:, hh, :], ps[:, 0:256]), [ps], [KxT])
    for mt in range(2):
        ps = psP.get()
        for ch in range(8):
            k.mm(ps, ps[:, :], mnT, mnT[:, ch, mt * 128:(mt + 1) * 128], wkv, wkv[:, ch, 512:1024], ch == 0, ch == 7)
        k.dve(lambda mt=mt, ps=ps: nc.vector.tensor_copy(Vx[:, mt, :], ps[:, :]), [ps], [Vx])

    hm_ring = Ring([fw.sbuf([128, 4, 1024], F32, "hm") for _ in range(1)])
    mT_ring = Ring([fw.sbuf([128, 8, 512], BF16, "mT") for _ in range(2)])
    hnT = fw.sbuf([128, 8, 512], BF16, "hnT")
    qT = fw.sbuf([128, 4, 512], BF16, "qT")
    oT = fw.sbuf([128, 4, 512], BF16, "oT")
    E_ring = Ring([fw.sbuf([128, 512], BF16, "E") for _ in range(3)])
    rcp = fw.sbuf([128, 512], F32, "rcp")
    scx = 128 ** -0.5
    for g in range(NGRP):
        t0 = g * 512
        xt = xt_ring.get()
        k.ld(xt, xt[:], hin[t0:t0 + 512, :].rearrange("(t p) d -> p t d", p=128))
        mT = mT_ring.get()
        k.ld(mT, mT[:], mt_full[:, t0:t0 + 512].rearrange("(c p) s -> p c s", p=128))
        hm = hm_ring.get()
        for t in range(4):
            for half in range(2):
                ps = psP.get()
                for ch in range(8):
                    k.mm(ps, ps[:, :], mT, mT[:, ch, t * 128:(t + 1) * 128], wo, wo[:, ch, half * 512:(half + 1) * 512],
                         ch == 0, ch == 7)
                k.dve(lambda t=t, half=half, ps=ps: nc.vector.tensor_tensor(
                    hm[:, t, half * 512:(half + 1) * 512], ps[:, :], xt[:, t, half * 512:(half + 1) * 512], op=ALU.add),
                    [ps, xt], [hm])
        rms_stats(k, hm, 4, ss, rstd, junk)
        norm_transpose(k, c, hm, 4, rstd, gxc, xs, hnT, psT)
        for hh in range(4):
            ps = psP.get()
            for ch in range(8):
                k.mm(ps, ps[:, :], wq, wq[:, ch, hh * 128:(hh + 1) * 128], hnT, hnT[:, ch, :], ch == 0, ch == 7)
            k.dve(lambda hh=hh, ps=ps: nc.vector.tensor_copy(qT[:, hh, :], ps[:, :]), [ps], [qT])
        for hh in range(4):
            for mt in range(2):
                ps = psP.get()
                k.mm(ps, ps[:, :], KxT, KxT[:, hh, mt * 128:(mt + 1) * 128], qT, qT[:, hh, :], True, True)
                Eb = E_ring.get()
                k.act(Eb, Eb[:], ps, ps[:], AF.Exp, scale=scx)
                k.mm(psA, psA[:, :], Vx, Vx[:, mt, hh * 128:(hh + 1) * 128], Eb, Eb[:, :], mt == 0, mt == 1)
                k.mm(psL, psL[:, :], c["ones"], c["ones"][:, :], Eb, Eb[:, :], mt == 0, mt == 1)
            k.dve(lambda: nc.vector.reciprocal(rcp[:], psL[:, :]), [psL], [rcp])
            k.dve(lambda hh=hh: nc.vector.tensor_tensor(oT[:, hh, :], psA[:, :], rcp[:], op=ALU.mult), [psA, rcp], [oT])
        for t in range(4):
            for half in range(2):
                ps = psP.get()
                for hh in range(4):
                    k.mm(ps, ps[:, :], oT, oT[:, hh, t * 128:(t + 1) * 128], wxo_s, wxo_s[:, hh, half * 512:(half + 1) * 512],
                         hh == 0, hh == 3)
                k.dve(lambda t=t, half=half, ps=ps: nc.vector.tensor_tensor(
                    hm[:, t, half * 512:(half + 1) * 512], ps[:, :], hm[:, t, half * 512:(half + 1) * 512], op=ALU.add),
                    [ps, hm], [hm])
        if final:
            rms_stats(k, hm, 4, ss, rstd, junk)
            for t in range(4):
                k.dve(lambda t=t: nc.vector.scalar_tensor_tensor(hm[:, t, :], hm[:, t, :], rstd[:, t:t + 1], gf[:],
                                                                 op0=ALU.mult, op1=ALU.mult), [hm, rstd, gf], [hm])
        k.st(hout[t0:t0 + 512, :].rearrange("(t p) d -> p t d", p=128), hm, hm[:])
    fw.barrier()
    fw.release(mark)


def _consts():
    bf = ml_dtypes.bfloat16
    kk = np.arange(128)[:, None]
    qq = np.arange(128)[None, :]
    c = {}
    c["ident"] = np.eye(128, dtype=np.float32).astype(bf)
    c["maskle"] = np.where(kk <= qq, 0.0, -BIG).astype(np.float32).astype(bf)
    c["masklt"] = np.where(kk < qq, 0.0, -BIG).astype(np.float32).astype(bf)
    c["maskltp"] = np.where(kk < qq, 0.0, BIG).astype(np.float32).astype(bf)
    c["tricum"] = (kk >= qq).astype(np.float32).astype(bf)
    c["ones"] = np.ones((128, 128), np.float32).astype(bf)
    c["zeros"] = np.zeros((128, 128), np.float32).astype(bf)
    oh = np.zeros((32, 32, 128), np.float32)
    for n in range(32):
        oh[n, n, :] = 1.0
    c["onehot"] = oh.reshape(32, 32 * 128).astype(bf)
    s = np.zeros((65, 64), np.float32)
    s[64, :] = 1.0
    c["sel65"] = s
    c["ones64"] = np.full((64, 64), 1.0 / 64, np.float32)
    rc = np.zeros((128, 4), np.float32)
    r = np.arange(128)
    r64 = r % 64
    inv64 = (1.0 / (np.float32(THETA) ** (np.arange(0, 16, 2, dtype=np.float32) / np.float32(16)))).astype(np.float32)
    inv32 = (1.0 / (np.float32(THETA) ** (np.arange(0, 8, 2, dtype=np.float32) / np.float32(8)))).astype(np.float32)
    rc[:, 0] = np.where(r64 < 16, inv64[r64 % 8], 0.0)
    rc[:, 1] = np.where(r64 < 8, -1.0, 1.0)
    r32 = r % 32
    rc[:, 2] = np.where(r32 < 8, inv32[r32 % 4], 0.0)
    rc[:, 3] = np.where(r32 < 4, -1.0, 1.0)
    c["ropec"] = rc
    return c


CONST_SPECS = [("ident", [128, 128], BF16), ("maskle", [128, 128], BF16), ("masklt", [128, 128], BF16),
               ("maskltp", [128, 128], BF16), ("tricum", [128, 128], BF16), ("ones", [128, 128], BF16),
               ("zeros", [128, 128], BF16), ("onehot", [32, 4096], BF16), ("sel65", [65, 64], F32), ("ones64", [64, 64], F32),
               ("ropec", [128, 4], F32)]


def _partner64(cols):
    out = []
    for c0 in cols:
        idx = np.arange(64)
        p = idx.copy()
        p[0:8] = idx[8:16]
        p[8:16] = idx[0:8]
        out.append(c0 + p)
    return np.concatenate(out)


def _partner32(cols):
    out = []
    for c0 in cols:
        idx = np.arange(32)
        p = idx.copy()
        p[0:4] = idx[4:8]
        p[4:8] = idx[0:4]
        out.append(c0 + p)
    return np.concatenate(out)


def _win_cols(g):
    MQ, MK, MV, MG, DQ, DK, DV, DG, SQ, SK, SV, SG = 0, 384, 768, 1152, 1536, 1792, 2048, 2304, 2560, 2944, 3328, 3712
    hm = [3 * g, 3 * g + 1, 3 * g + 2]
    hd = [2 * g, 2 * g + 1]
    r64 = np.arange(64)
    def blk(base, h, w=64):
        return base + h * w + np.arange(w)
    groups = []
    groups.append(np.concatenate([blk(MQ, hm[0]), blk(MQ, hm[1])]))
    groups.append(np.concatenate([blk(MK, hm[0]), blk(MK, hm[1])]))
    groups.append(np.concatenate([blk(MQ, hm[2]), blk(MK, hm[2])]))
    groups.append(_partner64([MQ + 64 * hm[0], MQ + 64 * hm[1]]))
    groups.append(_partner64([MK + 64 * hm[0], MK + 64 * hm[1]]))
    groups.append(_partner64([MQ + 64 * hm[2], MK + 64 * hm[2]]))
    groups.append(np.concatenate([blk(DQ, hd[0]), blk(DQ, hd[1])]))
    groups.append(np.concatenate([blk(DK, hd[0]), blk(DK, hd[1])]))
    groups.append(_partner32([DQ + 64 * hd[0], DQ + 64 * hd[0] + 32, DQ + 64 * hd[1], DQ + 64 * hd[1] + 32]))
    groups.append(_partner32([DK + 64 * hd[0], DK + 64 * hd[0] + 32, DK + 64 * hd[1], DK + 64 * hd[1] + 32]))
    groups.append(np.concatenate([blk(SQ, hm[0]), blk(SQ, hm[1])]))
    groups.append(np.concatenate([blk(SK, hm[0]), blk(SK, hm[1])]))
    groups.append(np.concatenate([blk(SQ, hm[2]), blk(SK, hm[2])]))
    gates = np.concatenate([blk(MG, hm[0]), blk(MG, hm[1]), blk(MG, hm[2]), blk(DG, hd[0]), blk(DG, hd[1]),
                            blk(SG, hm[0]), blk(SG, hm[1]), blk(SG, hm[2])])
    for i in range(4):
        groups.append(gates[i * 128:(i + 1) * 128])
    vcols = np.concatenate([blk(MV, hm[0]), blk(MV, hm[1]), blk(MV, hm[2]), blk(DV, hd[0]), blk(DV, hd[1]),
                            blk(SV, hm[0]), blk(SV, hm[1]), blk(SV, hm[2])])
    return np.concatenate(groups + [vcols])


def _merged_rows(g):
    return np.concatenate([np.arange(192 * g, 192 * g + 192), 384 + np.arange(128 * g, 128 * g + 128),
                           640 + np.arange(192 * g, 192 * g + 192)])


def _gcols(gvec):
    return np.ascontiguousarray(np.asarray(gvec, np.float32).reshape(8, 128).T)


def build_a(S, lam_init, debug=False):
    nc = bass.Bass("TRN2", target_bir_lowering=False)
    k = K(nc, S)
    cin = {n: nc.dram_tensor("c_" + n, s, d, kind="ExternalInput").ap() for n, s, d in CONST_SPECS}
    hin = nc.dram_tensor("hin", [S, D], F32, kind="ExternalInput").ap()
    pos = nc.dram_tensor("pos", [1, S], I32, kind="ExternalInput").ap()
    win = nc.dram_tensor("win", [D, NCOL], F32, kind="ExternalInput").ap()
    gin = nc.dram_tensor("gin", [128, 8], F32, kind="ExternalInput").ap()
    lamin = nc.dram_tensor("lamin", [1, 128], F32, kind="ExternalInput").ap()
    hng = nc.dram_tensor("hng", [64, 1], F32, kind="ExternalInput").ap()
    mt = nc.dram_tensor("mt", [512, S], BF16, kind="ExternalOutput").ap()
    kd = "ExternalOutput" if debug else "Internal"
    rope_d = nc.dram_tensor("rope_d", [4, 128, S], F32, kind=kd).ap()
    qk_d = nc.dram_tensor("qk_d", [QKROWS, S], BF16, kind=kd).ap()
    gt_d = nc.dram_tensor("gt_d", [512, S], BF16, kind=kd).ap()
    v_d = nc.dram_tensor("v_d", [S, 512], BF16, kind=kd).ap()
    rope_b, qk_b, gt_b, v_b, mt_b = Buf(None, "rope"), Buf(None, "qk"), Buf(None, "gt"), Buf(None, "v"), Buf(None, "mt")
    c = load_consts(k, cin)
    build_rope_tables(k, c, pos, rope_d, rope_b)
    stage_a_proj(k, c, hin, win, gin, rope_d, rope_b, qk_d, qk_b, gt_d, gt_b, v_d, v_b)
    stage_a_attn(k, c, lamin, hng, lam_init, qk_d, qk_b, gt_d, gt_b, v_d, v_b, mt, mt_b)
    k.fw.finish()
    k.fw.release(0)
    return nc


def build_b(ST, final):
    nc = bass.Bass("TRN2", target_bir_lowering=False)
    k = K(nc, ST)
    cin = {n: nc.dram_tensor("c_" + n, s, d, kind="ExternalInput").ap() for n, s, d in CONST_SPECS}
    hin = nc.dram_tensor("hin", [ST, D], F32, kind="ExternalInput").ap()
    mtf = nc.dram_tensor("mtf", [1024, ST], BF16, kind="ExternalInput").ap()
    wout = nc.dram_tensor("wout", [D, D], F32, kind="ExternalInput").ap()
    gx = nc.dram_tensor("gx", [128, 8], F32, kind="ExternalInput").ap()
    memin = nc.dram_tensor("memin", [MEM, D], F32, kind="ExternalInput").ap()
    gm = nc.dram_tensor("gm", [128, 8], F32, kind="ExternalInput").ap()
    wxq = nc.dram_tensor("wxq", [D, 512], F32, kind="ExternalInput").ap()
    wxkv = nc.dram_tensor("wxkv", [D, 1024], F32, kind="ExternalInput").ap()
    wxo = nc.dram_tensor("wxo", [512, D], F32, kind="ExternalInput").ap()
    gfin = nc.dram_tensor("gfin", [1, D], F32, kind="ExternalInput").ap()
    hout = nc.dram_tensor("hout", [ST, D], F32, kind="ExternalOutput").ap()
    c = load_consts(k, cin)
    stage_b(k, c, ST, hin, mtf, wout, gx, memin, gm, wxq, wxkv, wxo, gfin, hout, final)
    k.fw.finish()
    k.fw.release(0)
    return nc


def build_fused(S, depth=2):
    nc = bass.Bass("TRN2", target_bir_lowering=False)
    k = K(nc, S)
    cin = {n: nc.dram_tensor("c_" + n, s, d, kind="ExternalInput").ap() for n, s, d in CONST_SPECS}
    x = nc.dram_tensor("x", [S, D], F32, kind="ExternalInput").ap()
    pos = nc.dram_tensor("pos", [1, S], I32, kind="ExternalInput").ap()
    memin = nc.dram_tensor("memin", [MEM, D], F32, kind="ExternalInput").ap()
    gfin = nc.dram_tensor("gfin", [1, D], F32, kind="ExternalInput").ap()
    L = []
    for l in range(depth):
        d_ = {}
        for g in range(2):
            d_["win%d" % g] = nc.dram_tensor("win_%d_%d" % (l, g), [D, NCOL], F32, kind="ExternalInput").ap()
        d_["gin"] = nc.dram_tensor("gin_%d" % l, [128, 8], F32, kind="ExternalInput").ap()
        d_["lamin"] = nc.dram_tensor("lamin_%d" % l, [1, 128], F32, kind="ExternalInput").ap()
        d_["hng"] = nc.dram_tensor("hng_%d" % l, [64, 1], F32, kind="ExternalInput").ap()
        d_["wout"] = nc.dram_tensor("wout_%d" % l, [D, D], F32, kind="ExternalInput").ap()
        d_["gx"] = nc.dram_tensor("gx_%d" % l, [128, 8], F32, kind="ExternalInput").ap()
        d_["gm"] = nc.dram_tensor("gm_%d" % l, [128, 8], F32, kind="ExternalInput").ap()
        d_["wxq"] = nc.dram_tensor("wxq_%d" % l, [D, 512], F32, kind="ExternalInput").ap()
        d_["wxkv"] = nc.dram_tensor("wxkv_%d" % l, [D, 1024], F32, kind="ExternalInput").ap()
        d_["wxo"] = nc.dram_tensor("wxo_%d" % l, [512, D], F32, kind="ExternalInput").ap()
        L.append(d_)
    out = nc.dram_tensor("out", [S, D], F32, kind="ExternalOutput").ap()
    h1 = nc.dram_tensor("h1_d", [S, D], F32, kind="Internal").ap()
    mt = nc.dram_tensor("mt_d", [1024, S], BF16, kind="Internal").ap()
    rope_d = nc.dram_tensor("rope_d", [4, 128, S], F32, kind="Internal").ap()
    qk_d = nc.dram_tensor("qk_d", [QKROWS, S], BF16, kind="Internal").ap()
    gt_d = nc.dram_tensor("gt_d", [512, S], BF16, kind="Internal").ap()
    v_d = nc.dram_tensor("v_d", [S, 512], BF16, kind="Internal").ap()
    rope_b, qk_b, gt_b, v_b, mt_b = Buf(None, "rope"), Buf(None, "qk"), Buf(None, "gt"), Buf(None, "v"), Buf(None, "mt")
    c = load_consts(k, cin)
    build_rope_tables(k, c, pos, rope_d, rope_b)
    hin = x
    for l in range(depth):
        lam_init = 0.8 - 0.6 * math.exp(-0.3 * l)
        for g in range(2):
            stage_a_proj(k, c, hin, L[l]["win%d" % g], L[l]["gin"], rope_d, rope_b, qk_d, qk_b, gt_d, gt_b, v_d, v_b)
            stage_a_attn(k, c, L[l]["lamin"], L[l]["hng"], lam_init, qk_d, qk_b, gt_d, gt_b, v_d, v_b, mt, mt_b, mrow0=512 * g)
        final = (l == depth - 1)
        hout = out if final else h1
        stage_b(k, c, S, hin, mt, L[l]["wout"], L[l]["gx"], memin, L[l]["gm"], L[l]["wxq"], L[l]["wxkv"], L[l]["wxo"],
                gfin, hout, final)
        hin = h1
    k.fw.finish()
    k.fw.release(0)
    return nc

_CACHE = {}


def _get(key, fn):
    if key not in _CACHE:
        _CACHE[key] = fn()
    return _CACHE[key]


def kernel_unfused(x, mem, positions, attn_norm_g, w_in, w_out, diff_lambda_q1, diff_lambda_k1, diff_lambda_q2,
           diff_lambda_k2, diff_head_norm_g, xattn_norm_g, mem_norm_g, w_xq, w_xkv, w_xo, final_norm_g):
    x = np.asarray(x, np.float32)
    B, S, _ = x.shape
    consts = _consts()
    cmap = {"c_" + n: v for n, v in consts.items()}
    ncores = 2 * B
    SH = S // 2
    h = [np.ascontiguousarray(x[b]) for b in range(B)]
    depth = np.asarray(w_in).shape[0]
    for l in range(depth):
        lam_init = 0.8 - 0.6 * math.exp(-0.3 * l)
        nca = _get(("a", S, l), lambda: build_a(S, lam_init))
        in_maps = []
        for cidx in range(ncores):
            b, g = cidx // 2, cidx % 2
            lamin = np.concatenate([np.asarray(diff_lambda_q1[l]), np.asarray(diff_lambda_k1[l]),
                                    np.asarray(diff_lambda_q2[l]), np.asarray(diff_lambda_k2[l])]).astype(np.float32)[None, :]
            m = {"hin": h[b], "pos": np.ascontiguousarray(np.asarray(positions)[b:b + 1].astype(np.int32)),
                 "win": np.ascontiguousarray(np.asarray(w_in[l], np.float32)[:, _win_cols(g)]),
                 "gin": _gcols(attn_norm_g[l]), "lamin": lamin,
                 "hng": np.ascontiguousarray(np.asarray(diff_head_norm_g[l], np.float32).reshape(64, 1))}
            m.update(cmap)
            in_maps.append(m)
        res = run_bass_kernel_spmd(nca, in_maps, core_ids=list(range(ncores)))
        mts = [r["mt"] for r in res.results]
        final = (l == depth - 1)
        ncb = _get(("b", SH, final), lambda: build_b(SH, final))
        rows = np.concatenate([_merged_rows(0), _merged_rows(1)])
        wout_p = np.ascontiguousarray(np.asarray(w_out[l], np.float32)[rows, :])
        in_maps = []
        for cidx in range(ncores):
            b, g = cidx // 2, cidx % 2
            mtf = np.ascontiguousarray(np.concatenate([mts[2 * b][:, g * SH:(g + 1) * SH],
                                                       mts[2 * b + 1][:, g * SH:(g + 1) * SH]], axis=0))
            m = {"hin": np.ascontiguousarray(h[b][g * SH:(g + 1) * SH]), "mtf": mtf, "wout": wout_p,
                 "gx": _gcols(xattn_norm_g[l]), "memin": np.ascontiguousarray(np.asarray(mem[b], np.float32)),
                 "gm": _gcols(mem_norm_g[l]), "wxq": np.ascontiguousarray(np.asarray(w_xq[l], np.float32)),
                 "wxkv": np.ascontiguousarray(np.asarray(w_xkv[l], np.float32)),
                 "wxo": np.ascontiguousarray(np.asarray(w_xo[l], np.float32)),
                 "gfin": np.ascontiguousarray(np.asarray(final_norm_g, np.float32)[None, :])}
            m.update(cmap)
            in_maps.append(m)
        res = run_bass_kernel_spmd(ncb, in_maps, core_ids=list(range(ncores)))
        h = [np.concatenate([res.results[2 * b]["hout"], res.results[2 * b + 1]["hout"]], axis=0) for b in range(B)]
    return np.stack(h, axis=0).astype(np.float32)


def kernel(x, mem, positions, attn_norm_g, w_in, w_out, diff_lambda_q1, diff_lambda_k1, diff_lambda_q2,
           diff_lambda_k2, diff_head_norm_g, xattn_norm_g, mem_norm_g, w_xq, w_xkv, w_xo, final_norm_g):
    x = np.asarray(x, np.float32)
    B, S, _ = x.shape
    depth = np.asarray(w_in).shape[0]
    consts = _consts()
    base = {"c_" + n: v for n, v in consts.items()}
    base["gfin"] = np.ascontiguousarray(np.asarray(final_norm_g, np.float32)[None, :])
    rows = np.concatenate([_merged_rows(0), _merged_rows(1)])
    for l in range(depth):
        wl = np.asarray(w_in[l], np.float32)
        for g in range(2):
            base["win_%d_%d" % (l, g)] = np.ascontiguousarray(wl[:, _win_cols(g)])
        base["gin_%d" % l] = _gcols(attn_norm_g[l])
        base["lamin_%d" % l] = np.concatenate([np.asarray(diff_lambda_q1[l]), np.asarray(diff_lambda_k1[l]),
                                               np.asarray(diff_lambda_q2[l]), np.asarray(diff_lambda_k2[l])]).astype(np.float32)[None, :]
        base["hng_%d" % l] = np.ascontiguousarray(np.asarray(diff_head_norm_g[l], np.float32).reshape(64, 1))
        base["wout_%d" % l] = np.ascontiguousarray(np.asarray(w_out[l], np.float32)[rows, :])
        base["gx_%d" % l] = _gcols(xattn_norm_g[l])
        base["gm_%d" % l] = _gcols(mem_norm_g[l])
        base["wxq_%d" % l] = np.ascontiguousarray(np.asarray(w_xq[l], np.float32))
        base["wxkv_%d" % l] = np.ascontiguousarray(np.asarray(w_xkv[l], np.float32))
        base["wxo_%d" % l] = np.ascontiguousarray(np.asarray(w_xo[l], np.float32))
    ncores = 2 * B
    in_maps = []
    for cidx in range(ncores):
        b = cidx // 2
        m = dict(base)
        m["x"] = np.ascontiguousarray(x[b])
        m["pos"] = np.ascontiguousarray(np.asarray(positions)[b:b + 1].astype(np.int32))
        m["memin"] = np.ascontiguousarray(np.asarray(mem[b], np.float32))
        in_maps.append(m)
    nc = _get(("fused", S, depth), lambda: build_fused(S, depth))
    res = run_bass_kernel_spmd(nc, in_maps, core_ids=list(range(ncores)))
    SH = S // 2
    outs = [np.concatenate([res.results[2 * b]["out"][:SH], res.results[2 * b + 1]["out"][SH:]], axis=0) for b in range(B)]
    return np.stack(outs, axis=0).astype(np.float32)
```

```python
import math
import numpy as np
import ml_dtypes
import concourse.bass as bass
import concourse.mybir as mybir
from concourse.bass_utils import run_bass_kernel_spmd

F32 = mybir.dt.float32
BF16 = mybir.dt.bfloat16
I32 = mybir.dt.int32
AF = mybir.ActivationFunctionType
ALU = mybir.AluOpType
AX = mybir.AxisListType

D = 1024
SEQ = 8192
BATCH = 4
MEM = 256
EPS = 1e-6
THETA = 500000.0
BIG = 30000.0
NFM = 17
NCOL = NFM * 128 + 512
QKROWS = 1024


class Buf:
    __slots__ = ("ap", "lw", "rd", "name")

    def __init__(self, ap, name=""):
        self.ap = ap
        self.lw = {}
        self.rd = {}
        self.name = name

    def __getitem__(self, idx):
        return self.ap[idx]


class Engine:
    def __init__(self, fw, eng, name):
        self.fw = fw
        self.eng = eng
        self.name = name
        self.sem = fw.nc.alloc_semaphore(name="s_" + name)
        self.cnt = 0
        self.seen = {}

    def _wait(self, tok):
        sem, val, _ = tok
        k = id(sem)
        if self.seen.get(k, 0) >= val:
            return
        self.seen[k] = val
        self.eng.wait_ge(sem, val)

    def deps(self, reads, writes, acc=False):
        for b in reads:
            for t in b.lw.values():
                self._wait(t)
        for b in writes:
            for t in b.lw.values():
                if acc and t[2] == self.name:
                    continue
                self._wait(t)
            for t in b.rd.values():
                self._wait(t)

    def done(self, ins, reads, writes):
        self.cnt += 1
        ins.then_inc(self.sem, 1)
        tok = (self.sem, self.cnt, self.name)
        k = id(self.sem)
        for b in reads:
            b.rd[k] = tok
        for b in writes:
            b.lw[k] = tok
            b.rd = {}
        return tok

    def op(self, fn, reads=(), writes=(), acc=False):
        self.deps(reads, writes, acc)
        ins = fn()
        return self.done(ins, reads, writes)


class DmaQueue:
    def __init__(self, fw, engine, nsem, name):
        self.engine = engine
        self.sems = [fw.nc.alloc_semaphore(name=f"d_{name}{i}") for i in range(nsem)]
        self.cnts = [0] * nsem
        self.rr = 0
        self.name = name

    def dma(self, out_ap, in_ap, reads=(), writes=()):
        E = self.engine
        j = self.rr
        self.rr = (self.rr + 1) % len(self.sems)
        sem = self.sems[j]
        if self.cnts[j] > 0:
            E._wait((sem, 16 * self.cnts[j], "dma"))
        for b in reads:
            for t in b.lw.values():
                E._wait(t)
        for b in writes:
            for t in b.lw.values():
                E._wait(t)
            for t in b.rd.values():
                E._wait(t)
        ins = E.eng.dma_start(out=out_ap, in_=in_ap)
        self.cnts[j] += 1
        ins.then_inc(sem, 16)
        tok = (sem, 16 * self.cnts[j], "dma_" + self.name + str(j))
        k = id(sem)
        for b in reads:
            b.rd[k] = tok
        for b in writes:
            b.lw[k] = tok
            b.rd = {}
        return tok

    def all_tokens(self):
        return [(s, 16 * c, "dma") for s, c in zip(self.sems, self.cnts) if c > 0]


class FW:
    def __init__(self, nc):
        self.nc = nc
        self.pe = Engine(self, nc.tensor, "pe")
        self.act = Engine(self, nc.scalar, "act")
        self.dve = Engine(self, nc.vector, "dve")
        self.pool = Engine(self, nc.gpsimd, "pool")
        self.sp = Engine(self, nc.sync, "sp")
        self.engines = [self.pe, self.act, self.dve, self.pool, self.sp]
        self.q_ld = DmaQueue(self, self.sp, 12, "sp")
        self.q_st = DmaQueue(self, self.pool, 8, "pl")
        self._ctx = []
        self._n = 0

    def sbuf(self, shape, dtype, name=None):
        self._n += 1
        nm = (name or "sb") + str(self._n)
        g = self.nc.sbuf_tensor(nm, list(shape), dtype)
        t = g.__enter__()
        self._ctx.append(g)
        return Buf(t, nm)

    def psum(self, shape, dtype, name=None):
        self._n += 1
        nm = (name or "ps") + str(self._n)
        g = self.nc.psum_tensor(nm, list(shape), dtype)
        t = g.__enter__()
        self._ctx.append(g)
        return Buf(t, nm)

    def mark(self):
        return len(self._ctx)

    def release(self, mark):
        while len(self._ctx) > mark:
            self._ctx.pop().__exit__(None, None, None)

    def barrier(self):
        toks = [(e.sem, e.cnt, e.name) for e in self.engines if e.cnt > 0]
        toks += self.q_ld.all_tokens() + self.q_st.all_tokens()
        for e in self.engines:
            for t in toks:
                if t[0] is not e.sem:
                    e._wait(t)
        for e in self.engines:
            if e.cnt > 4000:
                e.sem = self.nc.alloc_semaphore(name="s_%s_%d" % (e.name, self._n))
                self._n += 1
                e.cnt = 0

    def finish(self):
        for t in self.q_ld.all_tokens() + self.q_st.all_tokens():
            self.sp._wait(t)


class Ring:
    def __init__(self, bufs):
        self.bufs = bufs
        self.i = 0

    def get(self):
        b = self.bufs[self.i]
        self.i = (self.i + 1) % len(self.bufs)
        return b


class K:
    def __init__(self, nc, S):
        self.nc = nc
        self.S = S
        self.fw = FW(nc)
        self.v = nc.vector
        self.a = nc.scalar
        self.g = nc.gpsimd
        self.t = nc.tensor

    def mm(self, ob, oap, lb, lap, rb, rap, start, stop):
        nc = self.nc
        self.fw.pe.op(lambda: nc.tensor.matmul(oap, lap, rap, start=start, stop=stop),
                      [lb, rb], [ob], acc=not start)

    def tr(self, ob, oap, ib, iap, identb):
        nc = self.nc
        self.fw.pe.op(lambda: nc.tensor.transpose(oap, iap, identb.ap[:]), [ib, identb], [ob], acc=True)

    def act(self, ob, oap, ib, iap, func, extra_reads=(), **kw):
        nc = self.nc
        self.fw.act.op(lambda: nc.scalar.activation(oap, iap, func, **kw), [ib] + list(extra_reads), [ob])

    def ld(self, ob, oap, src_ap, srcbuf=None):
        return self.fw.q_ld.dma(oap, src_ap, reads=[srcbuf] if srcbuf is not None else [], writes=[ob])

    def st(self, dst_ap, ib, iap, dstbuf=None):
        return self.fw.q_st.dma(dst_ap, iap, reads=[ib], writes=[dstbuf] if dstbuf is not None else [])

    def dve(self, fn, reads, writes):
        self.fw.dve.op(fn, reads, writes)

    def pool(self, fn, reads, writes):
        self.fw.pool.op(fn, reads, writes)


def load_consts(k, cin):
    fw, nc = k.fw, k.nc
    c = {}
    for name, shape, dt in [("ident", [128, 128], BF16), ("maskle", [128, 128], BF16),
                            ("masklt", [128, 128], BF16), ("maskltp", [128, 128], BF16),
                            ("tricum", [128, 128], BF16), ("ones", [128, 128], BF16), ("zeros", [128, 128], BF16),
                            ("onehot", [32, 32 * 128], BF16), ("sel65", [65, 64], F32),
                            ("ones64", [64, 64], F32), ("ropec", [128, 4], F32)]:
        b = fw.sbuf(shape, dt, "c_" + name)
        k.ld(b, b[:], cin[name][:, :])
        c[name] = b
    c["_onehot_dram"] = cin["onehot"]
    return c


def rms_stats(k, xt, ntile, ss, rstd, junk):
    nc = k.nc
    for t in range(ntile):
        k.fw.act.op(lambda t=t: nc.scalar.activation(junk[:], xt[:, t, :], AF.Square, accum_out=ss[:, t:t + 1]),
                    [xt], [junk, ss])
    k.fw.act.op(lambda: nc.scalar.activation(rstd[:, 0:ntile], ss[:, 0:ntile], AF.Sqrt, bias=EPS, scale=1.0 / D),
                [ss], [rstd])
    k.dve(lambda: nc.vector.reciprocal(rstd[:, 0:ntile], rstd[:, 0:ntile]), [rstd], [rstd])


def norm_transpose(k, c, xt, ntile, rstd, gcol, xs, hnT, psT_ring):
    nc = k.nc
    for t in range(ntile):
        k.dve(lambda t=t: nc.vector.tensor_scalar(xs[:, t, :], xt[:, t, :], rstd[:, t:t + 1], None, op0=ALU.mult),
              [xt, rstd], [xs])
    for ch in range(8):
        ps = psT_ring.get()
        for t in range(ntile):
            k.tr(ps, ps[:, t * 128:(t + 1) * 128], xs, xs[:, t, ch * 128:(ch + 1) * 128], c["ident"])
        if ch % 2 == 0:
            k.dve(lambda ch=ch, ps=ps: nc.vector.tensor_scalar(hnT[:, ch, 0:ntile * 128], ps[:, 0:ntile * 128],
                                                               gcol[:, ch:ch + 1], None, op0=ALU.mult),
                  [ps, gcol], [hnT])
        else:
            k.fw.act.op(lambda ch=ch, ps=ps: nc.scalar.activation(hnT[:, ch, 0:ntile * 128], ps[:, 0:ntile * 128],
                                                                  AF.Copy, scale=gcol[:, ch:ch + 1]),
                        [ps, gcol], [hnT])


def load_weights_bf16(k, wdram, ncols, wsb, stage_ring, col_chunk=512):
    nc = k.nc
    i = 0
    for ch in range(8):
        for c0 in range(0, ncols, col_chunk):
            cw = min(col_chunk, ncols - c0)
            stg = stage_ring.get()
            k.ld(stg, stg[:, 0:cw], wdram[ch * 128:(ch + 1) * 128, c0:c0 + cw])
            if i % 2 == 0:
                k.pool(lambda stg=stg, ch=ch, c0=c0, cw=cw: nc.gpsimd.tensor_copy(wsb[:, ch, c0:c0 + cw], stg[:, 0:cw]),
                       [stg], [wsb])
            else:
                k.dve(lambda stg=stg, ch=ch, c0=c0, cw=cw: nc.vector.tensor_copy(wsb[:, ch, c0:c0 + cw], stg[:, 0:cw]),
                      [stg], [wsb])
            i += 1


def build_rope_tables(k, c, pos_dram, rope_dram, rope_buf):
    nc, S = k.nc, k.S
    fw = k.fw
    mark = fw.mark()
    CH = min(S, 2048)
    pi_ = fw.sbuf([128, CH], I32, "posi")
    pf = fw.sbuf([128, CH], F32, "posf")
    ang = fw.sbuf([128, CH], F32, "ang")
    t2 = fw.sbuf([128, CH], F32, "t2")
    o = Ring([fw.sbuf([128, CH], F32, "ro") for _ in range(2)])
    MAG = 12582912.0
    for c0 in range(0, S, CH):
        src = pos_dram[0:1, c0:c0 + CH]
        k.ld(pi_, pi_[:], src.partition_broadcast(128))
        k.act(pf, pf[:], pi_, pi_[:], AF.Copy)
        for pat in range(2):
            inv = c["ropec"][:, 2 * pat:2 * pat + 1]
            sgn = c["ropec"][:, 2 * pat + 1:2 * pat + 2]
            for which in range(2):
                k.dve(lambda inv=inv, which=which: nc.vector.tensor_scalar(
                    ang[:], pf[:], inv, (math.pi / 2 if which == 0 else 0.0), op0=ALU.mult, op1=ALU.add),
                    [pf, c["ropec"]], [ang])
                k.dve(lambda: nc.vector.tensor_scalar(t2[:], ang[:], 1.0 / (2 * math.pi), MAG, op0=ALU.mult, op1=ALU.add),
                      [ang], [t2])
                k.dve(lambda: nc.vector.tensor_scalar(t2[:], t2[:], MAG, -2 * math.pi, op0=ALU.subtract, op1=ALU.mult),
                      [t2], [t2])
                k.dve(lambda: nc.vector.tensor_tensor(ang[:], ang[:], t2[:], op=ALU.add), [ang, t2], [ang])
                k.dve(lambda: nc.vector.tensor_scalar(ang[:], ang[:], -3.141592, 3.141592, op0=ALU.max, op1=ALU.min),
                      [ang], [ang])
                ob = o.get()
                if which == 0:
                    k.act(ob, ob[:], ang, ang[:], AF.Sin)
                else:
                    k.act(t2, t2[:], ang, ang[:], AF.Sin)
                    k.dve(lambda ob=ob, sgn=sgn: nc.vector.tensor_scalar(ob[:], t2[:], sgn, None, op0=ALU.mult),
                          [t2, c["ropec"]], [ob])
                k.st(rope_dram[2 * pat + which, :, c0:c0 + CH], ob, ob[:], rope_buf)
    fw.barrier()
    fw.release(mark)


def stage_a_proj(k, c, hin, win, gin, rope_dram, rope_buf, qk_dram, qk_buf, gt_dram, gt_buf, v_dram, v_buf):
    nc, S, fw = k.nc, k.S, k.fw
    mark = fw.mark()
    NGRP = S // 512
    wsb = fw.sbuf([128, 8, NCOL], BF16, "wsb")
    stage_ring = Ring([fw.sbuf([128, 512], F32, "wstg") for _ in range(3)])
    gcol = fw.sbuf([128, 8], F32, "gcol")
    k.ld(gcol, gcol[:], gin[:, :])
    load_weights_bf16(k, win, NCOL, wsb, stage_ring)
    xt_ring = Ring([fw.sbuf([128, 4, 1024], F32, "xt") for _ in range(2)])
    xs = fw.sbuf([128, 4, 1024], BF16, "xs")
    hnT_ring = Ring([fw.sbuf([128, 8, 512], BF16, "hnT") for _ in range(2)])
    junk = fw.sbuf([128, 1024], BF16, "junk")
    ss = fw.sbuf([128, 8], F32, "ss")
    rstd = fw.sbuf([128, 8], F32, "rstd")
    rope_ring = Ring([fw.sbuf([128, 4, 512], F32, "rope") for _ in range(2)])
    tmp_ring = Ring([fw.sbuf([128, 512], F32, "ptmp") for _ in range(4)])
    ob_ring = Ring([fw.sbuf([128, 512], BF16, "pob") for _ in range(6)])
    vo_ring = Ring([fw.sbuf([128, 512], BF16, "vob") for _ in range(3)])
    psT = Ring([fw.psum([128, 512], BF16, "psT") for _ in range(2)])
    psP = Ring([fw.psum([128, 512], F32, "psP") for _ in range(5)])

    fm = [(0, 3, 0, 0), (1, 4, 0, 128), (2, 5, 0, 256),
          (6, 8, 1, 384), (7, 9, 1, 512),
          (10, None, 2, 640), (11, None, 2, 768), (12, None, 2, 896)]
    for g in range(NGRP):
        t0 = g * 512
        xt = xt_ring.get()
        k.ld(xt, xt[:], hin[t0:t0 + 512, :].rearrange("(t p) d -> p t d", p=128))
        rp = rope_ring.get()
        k.ld(rp, rp[:], rope_dram[:, :, t0:t0 + 512].rearrange("f p s -> p f s"), rope_buf)
        rms_stats(k, xt, 4, ss, rstd, junk)
        hnT = hnT_ring.get()
        norm_transpose(k, c, xt, 4, rstd, gcol, xs, hnT, psT)

        def proj(cg):
            ps = psP.get()
            for ch in range(8):
                k.mm(ps, ps[:, :], wsb, wsb[:, ch, cg * 128:(cg + 1) * 128], hnT, hnT[:, ch, :], ch == 0, ch == 7)
            return ps

        for (cg, pg, kind, row0) in fm:
            ps = proj(cg)
            ob = ob_ring.get()
            if pg is None:
                if cg % 2 == 0:
                    k.dve(lambda ob=ob, ps=ps: nc.vector.tensor_copy(ob[:], ps[:]), [ps], [ob])
                else:
                    k.act(ob, ob[:], ps, ps[:], AF.Copy)
            else:
                ps2 = proj(pg)
                t1 = tmp_ring.get()
                t2 = tmp_ring.get()
                k.dve(lambda t1=t1, ps=ps, kind=kind: nc.vector.tensor_tensor(t1[:], ps[:], rp[:, 2 * kind, :], op=ALU.mult),
                      [ps, rp], [t1])
                k.dve(lambda t2=t2, ps2=ps2, kind=kind: nc.vector.tensor_tensor(t2[:], ps2[:], rp[:, 2 * kind + 1, :], op=ALU.mult),
                      [ps2, rp], [t2])
                k.pool(lambda ob=ob, t1=t1, t2=t2: nc.gpsimd.tensor_tensor(ob[:], t1[:], t2[:], op=ALU.add), [t1, t2], [ob])
            k.st(qk_dram[row0:row0 + 128, t0:t0 + 512], ob, ob[:], qk_buf)
        for gi in range(4):
            ps = proj(13 + gi)
            ob = ob_ring.get()
            k.act(ob, ob[:], ps, ps[:], AF.Silu)
            k.st(gt_dram[gi * 128:(gi + 1) * 128, t0:t0 + 512], ob, ob[:], gt_buf)
        for t in range(4):
            ps = psP.get()
            for ch in range(8):
                k.mm(ps, ps[:, :], hnT, hnT[:, ch, t * 128:(t + 1) * 128], wsb, wsb[:, ch, NFM * 128:NFM * 128 + 512],
                     ch == 0, ch == 7)
            vo = vo_ring.get()
            k.dve(lambda vo=vo, ps=ps: nc.vector.tensor_copy(vo[:], ps[:]), [ps], [vo])
            k.st(v_dram[t0 + t * 128:t0 + (t + 1) * 128, :], vo, vo[:], v_buf)
    fw.barrier()
    fw.release(mark)


def stage_a_attn(k, c, lamin, hng, lam_init, qk_dram, qk_buf, gt_dram, gt_buf, v_dram, v_buf, mt_dram, mt_buf, mrow0=0):
    nc, S, fw = k.nc, k.S, k.fw
    mark = fw.mark()
    NG = S // 512
    NT = S // 128
    NB = S // 256
    ident, ones = c["ident"], c["ones"]
    cin_onehot = c["_onehot_dram"]

    KT = fw.sbuf([128, S], BF16, "KT")
    KN = fw.sbuf([128, S], BF16, "KN")
    QT = fw.sbuf([128, S], BF16, "QT")
    K2 = fw.sbuf([128, S], BF16, "K2")
    Vt = fw.sbuf([128, NT, 128], BF16, "Vt")
    MT = fw.sbuf([32, S], BF16, "MT")
    E_ring = Ring([fw.sbuf([128, 512], BF16, "E") for _ in range(4)])
    E32_ring = Ring([fw.sbuf([128, 512], F32, "E32") for _ in range(2)])
    SP_ring = Ring([fw.sbuf([128, 512], BF16, "SP") for _ in range(3)])
    R = fw.sbuf([128, 512], BF16, "R")
    ob32 = Ring([fw.sbuf([65, 512], F32, "o32") for _ in range(3)])
    rc = Ring([fw.sbuf([64, 512], F32, "rc") for _ in range(3)])
    gtile = Ring([fw.sbuf([64, 512], BF16, "gt") for _ in range(2)])
    mout = Ring([fw.sbuf([64, 512], BF16, "mo") for _ in range(2)])
    kmf = fw.sbuf([64, 32], F32, "kmf")
    kmb = fw.sbuf([64, 32], BF16, "kmb")
    G32 = fw.sbuf([128, 32], F32, "G32")
    m8 = fw.sbuf([128, 8], F32, "m8")
    Mb = fw.sbuf([128, 96], BF16, "Mb")
    lamv = fw.sbuf([64, 128], F32, "lamv")
    lamj = fw.sbuf([64, 32], F32, "lamj")
    lam2 = fw.sbuf([64, 4], F32, "lam2")
    lamc = fw.sbuf([64, 1], F32, "lamc")
    gsc = fw.sbuf([64, 1], F32, "gsc")
    psW = Ring([fw.psum([128, 512], F32, "psW") for _ in range(4)])
    psO1 = fw.psum([128, 512], F32, "psO1")
    psO2 = fw.psum([128, 512], F32, "psO2")
    psX = fw.psum([128, 512], F32, "psX")
    psXb = fw.psum([128, 512], BF16, "psXb")

    E_zero = fw.sbuf([128, 512], BF16, "Ezero")
    k.pool(lambda: nc.gpsimd.memset(E_zero[:], 0.0), [], [E_zero])
    k.pool(lambda: nc.gpsimd.memset(Vt[:, :, 64:128], 1.0), [], [Vt])

    k.ld(lamv, lamv[:], lamin[0:1, :].partition_broadcast(64))
    k.ld(gsc, gsc[:], hng[:, :])
    for i in range(2):
        k.dve(lambda i=i: nc.vector.tensor_tensor(lamj[:], lamv[:, 64 * i:64 * i + 32], lamv[:, 64 * i + 32:64 * i + 64], op=ALU.mult),
              [lamv], [lamj])
        k.dve(lambda i=i: nc.vector.reduce_sum(lam2[:, i:i + 1], lamj[:], axis=AX.X), [lamj], [lam2])
    k.act(lam2, lam2[:, 2:4], lam2, lam2[:, 0:2], AF.Exp)
    k.dve(lambda: nc.vector.tensor_tensor(lamc[:], lam2[:, 3:4], lam2[:, 2:3], op=ALU.subtract), [lam2], [lamc])
    k.dve(lambda: nc.vector.tensor_scalar_add(lamc[:], lamc[:], -lam_init), [lamc], [lamc])
    k.dve(lambda: nc.vector.tensor_scalar_mul(gsc[:], gsc[:], 1.0 - lam_init), [gsc], [gsc])

    def load_head(qrow, krow, vcol, need_q):
        for s0 in range(0, S, 2048):
            s1 = min(S, s0 + 2048)
            k.ld(KT, KT[0:64, s0:s1], qk_dram[krow:krow + 64, s0:s1], qk_buf)
            if need_q:
                k.ld(QT, QT[0:64, s0:s1], qk_dram[qrow:qrow + 64, s0:s1], qk_buf)
        for t0 in range(0, NT, 16):
            t1 = min(NT, t0 + 16)
            k.ld(Vt, Vt[:, t0:t1, 0:64],
                 v_dram[t0 * 128:t1 * 128, vcol:vcol + 64].rearrange("(t p) d -> p t d", p=128), v_buf)

    def finish_group(kind, g, grow, mrow, o_list):
        t0 = g * 512
        gt_ = gtile.get()
        k.ld(gt_, gt_[:], gt_dram[grow:grow + 64, t0:t0 + 512], gt_buf)
        mo = mout.get()
        if kind == "sb":
            k.dve(lambda: nc.vector.tensor_tensor(mo[:], o_list[0][0:64, :], gt_[:], op=ALU.mult), [o_list[0], gt_], [mo])
        else:
            res = []
            for po in o_list:
                o32 = ob32.get()
                k.dve(lambda o32=o32, po=po: nc.vector.tensor_copy(o32[:], po[0:65, :]), [po], [o32])
                k.mm(psX, psX[0:64, :], c["sel65"], c["sel65"][:, :], o32, o32[:, :], True, True)
                r = rc.get()
                k.dve(lambda r=r: nc.vector.reciprocal(r[:], psX[0:64, :]), [psX], [r])
                k.dve(lambda r=r, o32=o32: nc.vector.tensor_tensor(r[:], r[:], o32[0:64, :], op=ALU.mult), [r, o32], [r])
                res.append(r)
            if kind == "moba":
                k.dve(lambda: nc.vector.tensor_tensor(mo[:], res[0][:], gt_[:], op=ALU.mult), [res[0], gt_], [mo])
            else:
                a = res[0]
                k.dve(lambda: nc.vector.scalar_tensor_tensor(a[:], res[1][:], lamc[:, 0:1], a[:], op0=ALU.mult, op1=ALU.add),
                      [res[1], a, lamc], [a])
                sq = res[1]
                k.dve(lambda: nc.vector.tensor_tensor(sq[:], a[:], a[:], op=ALU.mult), [a], [sq])
                k.mm(psX, psX[0:64, :], c["ones64"], c["ones64"][:, :], sq, sq[:, :], True, True)
                k.act(sq, sq[:], psX, psX[0:64, :], AF.Ln, bias=EPS)
                k.act(sq, sq[:], sq, sq[:], AF.Exp, scale=-0.5)
                k.dve(lambda: nc.vector.scalar_tensor_tensor(a[:], a[:], gsc[:, 0:1], sq[:], op0=ALU.mult, op1=ALU.mult),
                      [a, gsc, sq], [a])
                k.dve(lambda: nc.vector.tensor_tensor(mo[:], a[:], gt_[:], op=ALU.mult), [a, gt_], [mo])
        k.st(mt_dram[mrow0 + mrow:mrow0 + mrow + 64, t0:t0 + 512], mo, mo[:], mt_buf)

    def zinit(po):
        k.mm(po, po[:, :], c["zeros"], c["zeros"][:, :], E_zero, E_zero[:, :], True, False)

    def pv(po, kb, g, Eb, sub_first_is_start, descending):
        j = kb - 4 * g
        last_kb = (kb == 0) if descending else (kb == 4 * g + 3)
        if j < 0:
            k.mm(po, po[:, :], Vt, Vt[:, kb, :], Eb, Eb[:, :], False, last_kb)
        else:
            for s in range(j, 4):
                k.mm(po, po[:, s * 128:(s + 1) * 128], Vt, Vt[:, kb, :], Eb, Eb[:, s * 128:(s + 1) * 128], False,
                     last_kb and s == 3)

    def run_pipeline(items, stages):
        n = len(items)
        mx = max(sk for sk, _ in stages)
        for step in range(n + mx):
            for sk, fn in stages:
                i = step - sk
                if 0 <= i < n:
                    fn(items[i])

    Rb = [R, fw.sbuf([128, 512], BF16, "R2")]

    sc64 = 64 ** -0.5
    for h in range(3):
        qrow = [0, 64, 256][h]
        krow = [128, 192, 320][h]
        load_head(qrow, krow, 64 * h, True)
        k.dve(lambda: nc.vector.tensor_reduce(kmf[:, 0:NB], KT[0:64, :].rearrange("p (n l) -> p n l", l=256), axis=AX.X, op=ALU.add),
              [KT], [kmf])
        k.dve(lambda: nc.vector.tensor_scalar_mul(kmb[:, 0:NB], kmf[:, 0:NB], 1.0 / 256), [kmf], [kmb])
        if h == 0:
            ohv = cin_onehot.rearrange("p (n i) -> p n i", i=128)
            for two in range(2):
                k.ld(KT, KT[64:96, :].rearrange("p (n two i) -> p n two i", two=2, i=128)[:, :, two, :], ohv[:, 0:NB, :])
            k.pool(lambda: nc.gpsimd.memset(Mb[:, 0:64], 0.0), [], [Mb])
        for qt in range(NT):
            own = qt // 2
            k.pool(lambda: nc.gpsimd.memset(Mb[:, 64:96], -BIG), [], [Mb])
            k.pool(lambda own=own: nc.gpsimd.memset(Mb[:, 64 + own:65 + own], 0.0), [], [Mb])
            if own > 0:
                k.mm(psX, psX[:, 0:own], QT, QT[0:64, qt * 128:(qt + 1) * 128], kmb, kmb[:, 0:own], True, True)
                k.dve(lambda: nc.vector.memset(G32[:], -1e30), [], [G32])
                k.dve(lambda own=own: nc.vector.tensor_copy(G32[:, 0:own], psX[:, 0:own]), [psX], [G32])
                k.dve(lambda: nc.vector.max(m8[:], G32[:]), [G32], [m8])
                k.dve(lambda own=own: nc.vector.tensor_scalar(Mb[:, 64:64 + own], G32[:, 0:own], m8[:, 2:3], -BIG,
                                                              op0=ALU.is_lt, op1=ALU.mult), [G32, m8], [Mb])
            k.tr(psXb, psXb[0:96, 0:128], Mb, Mb[:, :], ident)
            k.dve(lambda qt=qt: nc.vector.tensor_copy(QT[64:96, qt * 128:(qt + 1) * 128], psXb[64:96, 0:128]), [psXb], [QT])
        items = []
        for g in range(NG):
            for kb in range(4 * g + 4):
                j = kb - 4 * g
                items.append(dict(g=g, kb=kb, j=j, q0=(128 * j if j > 0 else 0), first=(kb == 0), last=(kb == 4 * g + 3),
                                  po=(psO1 if g % 2 == 0 else psO2)))

        def m_s(it):
            g, kb, j, q0 = it["g"], it["kb"], it["j"], it["q0"]
            n = kb // 2
            ps = psW.get()
            it["ps"] = ps
            k.mm(ps, ps[:, q0:512], KT, KT[0:96, kb * 128:(kb + 1) * 128], QT, QT[0:96, g * 512 + q0:(g + 1) * 512], True, j < 0)
            if j >= 0:
                k.mm(ps, ps[:, q0:q0 + 128], ident, ident[:, :], c["maskle"], c["maskle"][:, :], False, True)

        def m_e(it):
            q0, ps = it["q0"], it["ps"]
            Eb = E_ring.get()
            it["E"] = Eb
            k.act(Eb, Eb[:, q0:512], ps, ps[:, q0:512], AF.Exp, scale=sc64)

        def m_pv(it, h=h):
            if it["first"]:
                zinit(it["po"])
            pv(it["po"], it["kb"], it["g"], it["E"], True, False)
            if it["last"]:
                finish_group("moba", it["g"], 64 * h, 64 * h, [it["po"]])

        run_pipeline(items, [(0, m_s), (0, m_e), (2, m_pv)])

    sc32 = 32 ** -0.5
    for h in range(2):
        qrow = 384 + 64 * h
        krow = 512 + 64 * h
        for s0 in range(0, S, 2048):
            s1 = min(S, s0 + 2048)
            for r in range(4):
                k.ld(KT, KT[32 * r:32 * r + 32, s0:s1], qk_dram[krow:krow + 32, s0:s1], qk_buf)
                k.ld(K2, K2[32 * r:32 * r + 32, s0:s1], qk_dram[krow + 32:krow + 64, s0:s1], qk_buf)
                k.ld(QT, QT[32 * r:32 * r + 32, s0:s1], qk_dram[qrow:qrow + 32, s0:s1], qk_buf)
                k.ld(KN, KN[32 * r:32 * r + 32, s0:s1], qk_dram[qrow + 32:qrow + 64, s0:s1], qk_buf)
        for t0_ in range(0, NT, 16):
            t1_ = min(NT, t0_ + 16)
            k.ld(Vt, Vt[:, t0_:t1_, 0:64],
                 v_dram[t0_ * 128:t1_ * 128, 192 + 64 * h:256 + 64 * h].rearrange("(t p) d -> p t d", p=128), v_buf)
        items = []
        for g in range(NG):
            for kb in range(4 * g + 4):
                j = kb - 4 * g
                for cc in range(2):
                    items.append(dict(g=g, kb=kb, j=j, q0=(128 * j if j > 0 else 0), cc=cc, first=(kb == 0 and cc == 0),
                                      last=(kb == 4 * g + 3 and cc == 1)))

        def d_s(it):
            g, kb, j, q0, cc = it["g"], it["kb"], it["j"], it["q0"], it["cc"]
            ps = psW.get()
            it["ps"] = ps
            Qz = QT if cc == 0 else KN
            Kz = KT if cc == 0 else K2
            k.mm(ps, ps[:, q0:512], Kz, Kz[:, kb * 128:(kb + 1) * 128], Qz, Qz[:, g * 512 + q0:(g + 1) * 512], True, j < 0)
            if j >= 0:
                k.mm(ps, ps[:, q0:q0 + 128], ident, ident[:, :], c["maskle"], c["maskle"][:, :], False, True)

        def d_e(it):
            q0, ps = it["q0"], it["ps"]
            Eb = E_ring.get()
            it["E"] = Eb
            k.act(Eb, Eb[:, q0:512], ps, ps[:, q0:512], AF.Exp, scale=sc32 / 4.0)

        def d_pv(it, h=h):
            if it["first"]:
                zinit(psO1)
                zinit(psO2)
            pv(psO1 if it["cc"] == 0 else psO2, it["kb"], it["g"], it["E"], True, False)
            if it["last"]:
                finish_group("diff", it["g"], 192 + 64 * h, 192 + 64 * h, [psO1, psO2])

        run_pipeline(items, [(0, d_s), (0, d_e), (2, d_pv)])

    for h in range(3):
        qrow = [640, 704, 896][h]
        krow = [768, 832, 960][h]
        load_head(qrow, krow, 320 + 64 * h, True)
        k.dve(lambda: nc.vector.tensor_scalar(KN[0:64, :], KT[0:64, :], -sc64, None, op0=ALU.mult), [KT], [KN])
        items = []
        for g in range(NG):
            nkb = 4 * g + 4
            for idx, kb in enumerate(range(nkb - 1, -1, -1)):
                j = kb - 4 * g
                items.append(dict(g=g, kb=kb, j=j, q0=(128 * j if j > 0 else 0), idx=idx, first=(idx == 0), last=(kb == 0),
                                  po=(psO1 if g % 2 == 0 else psO2)))

        def s_z(it):
            g, kb, j, q0 = it["g"], it["kb"], it["j"], it["q0"]
            qsl = slice(g * 512 + q0, (g + 1) * 512)
            pz = psW.get()
            it["pz"] = pz
            k.mm(pz, pz[:, q0:512], KT, KT[0:64, kb * 128:(kb + 1) * 128], QT, QT[0:64, qsl], True, j < 0)
            if j >= 0:
                k.mm(pz, pz[:, q0:q0 + 128], ident, ident[:, :], c["masklt"], c["masklt"][:, :], False, True)

        def s_sp(it):
            q0, pz = it["q0"], it["pz"]
            E32 = E32_ring.get()
            k.act(E32, E32[:, q0:512], pz, pz[:, q0:512], AF.Exp, scale=sc64)
            SPb = SP_ring.get()
            it["SP"] = SPb
            k.act(SPb, SPb[:, q0:512], E32, E32[:, q0:512], AF.Ln, bias=1.0)

        def s_u(it):
            g, kb, j, q0, idx, SPb = it["g"], it["kb"], it["j"], it["q0"], it["idx"], it["SP"]
            qsl = slice(g * 512 + q0, (g + 1) * 512)
            Rp = Rb[(idx + 1) % 2]
            Rn = Rb[idx % 2]
            pu = psW.get()
            it["pu"] = pu
            k.mm(pu, pu[:, q0:512], KN, KN[0:64, kb * 128:(kb + 1) * 128], QT, QT[0:64, qsl], True, False)
            lo = q0 + 128 if j >= 0 else 0
            if idx > 0 and lo < 512:
                k.mm(pu, pu[:, lo:512], ones, ones[:, :], Rp, Rp[:, lo:512], False, False)
            if j >= 0:
                k.mm(pu, pu[:, q0:q0 + 128], ident, ident[:, :], c["maskltp"], c["maskltp"][:, :], False, False)
            k.mm(pu, pu[:, q0:512], c["tricum"], c["tricum"][:, :], SPb, SPb[:, q0:512], False, True)
            if kb > 0:
                if j >= 0:
                    k.dve(lambda: nc.vector.tensor_copy(Rn[:, q0:q0 + 128], SPb[:, q0:q0 + 128]), [SPb], [Rn])
                    if lo < 512:
                        if idx > 0:
                            k.dve(lambda: nc.vector.tensor_tensor(Rn[:, lo:512], Rp[:, lo:512], SPb[:, lo:512], op=ALU.add),
                                  [SPb, Rp], [Rn])
                        else:
                            k.dve(lambda: nc.vector.tensor_copy(Rn[:, lo:512], SPb[:, lo:512]), [SPb], [Rn])
                else:
                    k.dve(lambda: nc.vector.tensor_tensor(Rn[:, :], Rp[:, :], SPb[:, :], op=ALU.add), [SPb, Rp], [Rn])

        def s_a(it):
            q0, pu = it["q0"], it["pu"]
            Ab = E_ring.get()
            it["A"] = Ab
            k.act(Ab, Ab[:, q0:512], pu, pu[:, q0:512], AF.Exp, scale=-1.0)

        def s_pv(it, h=h):
            if it["first"]:
                zinit(it["po"])
            pv(it["po"], it["kb"], it["g"], it["A"], True, True)
            if it["last"]:
                finish_group("sb", it["g"], 320 + 64 * h, 320 + 64 * h, [it["po"]])

        run_pipeline(items, [(0, s_z), (0, s_sp), (1, s_u), (1, s_a), (2, s_pv)])
    fw.barrier()
    fw.release(mark)


def stage_b(k, c, ST, hin, mt_full, wout, gx, memin, gm, wxq, wxkv, wxo, gfin, hout, final):
    nc, fw = k.nc, k.fw
    mark = fw.mark()
    ident = c["ident"]
    NGRP = ST // 512
    stage_ring = Ring([fw.sbuf([128, 512], F32, "wstg") for _ in range(3)])
    wo = fw.sbuf([128, 8, 1024], BF16, "wo")
    wq = fw.sbuf([128, 8, 512], BF16, "wq")
    wkv = fw.sbuf([128, 8, 1024], BF16, "wkv")
    wxo_s = fw.sbuf([128, 4, 1024], BF16, "wxo")
    load_weights_bf16(k, wout, 1024, wo, stage_ring)
    load_weights_bf16(k, wxq, 512, wq, stage_ring)
    load_weights_bf16(k, wxkv, 1024, wkv, stage_ring)
    for ch in range(4):
        for c0 in (0, 512):
            stg = stage_ring.get()
            k.ld(stg, stg[:, :], wxo[ch * 128:(ch + 1) * 128, c0:c0 + 512])
            k.dve(lambda stg=stg, ch=ch, c0=c0: nc.vector.tensor_copy(wxo_s[:, ch, c0:c0 + 512], stg[:, :]), [stg], [wxo_s])
    gxc = fw.sbuf([128, 8], F32, "gxc")
    gmc = fw.sbuf([128, 8], F32, "gmc")
    k.ld(gxc, gxc[:], gx[:, :])
    k.ld(gmc, gmc[:], gm[:, :])
    gf = fw.sbuf([128, 1024], F32, "gf")
    if final:
        k.ld(gf, gf[:], gfin[0:1, :].partition_broadcast(128))
    junk = fw.sbuf([128, 1024], BF16, "junk")
    ss = fw.sbuf([128, 8], F32, "ss")
    rstd = fw.sbuf([128, 8], F32, "rstd")
    xs = fw.sbuf([128, 4, 1024], BF16, "xs")
    psT = Ring([fw.psum([128, 512], BF16, "psT") for _ in range(2)])
    psP = Ring([fw.psum([128, 512], F32, "psP") for _ in range(4)])
    psA = fw.psum([128, 512], F32, "psA")
    psL = fw.psum([128, 512], F32, "psL")

    xt_ring = Ring([fw.sbuf([128, 4, 1024], F32, "xt") for _ in range(2)])
    memt = xt_ring.bufs[0]
    k.ld(memt, memt[:, 0:2, :], memin[:, :].rearrange("(t p) d -> p t d", p=128))
    rms_stats(k, memt, 2, ss, rstd, junk)
    mnT = fw.sbuf([128, 8, 512], BF16, "mnT")
    norm_transpose(k, c, memt, 2, rstd, gmc, xs, mnT, psT)
    KxT = fw.sbuf([128, 4, 256], BF16, "KxT")
    Vx = fw.sbuf([128, 2, 512], BF16, "Vx")
    for hh in range(4):
        ps = psP.get()
        for ch in range(8):
            k.mm(ps, ps[:, 0:256], wkv, wkv[:, ch, hh * 128:(hh + 1) * 128], mnT, mnT[:, ch, 0:256], ch == 0, ch == 7)
        k.dve(lambda hh=hh, ps=ps: nc.vector.tensor_copy(KxT[:, hh, :], ps[:, 0:256]), [ps], [KxT])
    for mt in range(2):
        ps = psP.get()
        for ch in range(8):
            k.mm(ps, ps[:, :], mnT, mnT[:, ch, mt * 128:(mt + 1) * 128], wkv, wkv[:, ch, 512:1024], ch == 0, ch == 7)
        k.dve(lambda mt=mt, ps=ps: nc.vector.tensor_copy(Vx[:, mt, :], ps[:, :]), [ps], [Vx])

    hm_ring = Ring([fw.sbuf([128, 4, 1024], F32, "hm") for _ in range(1)])
    mT_ring = Ring([fw.sbuf([128, 8, 512], BF16, "mT") for _ in range(2)])
    hnT = fw.sbuf([128, 8, 512], BF16, "hnT")
    qT = fw.sbuf([128, 4, 512], BF16, "qT")
    oT = fw.sbuf([128, 4, 512], BF16, "oT")
    E_ring = Ring([fw.sbuf([128, 512], BF16, "E") for _ in range(3)])
    rcp = fw.sbuf([128, 512], F32, "rcp")
    scx = 128 ** -0.5
    for g in range(NGRP):
        t0 = g * 512
        xt = xt_ring.get()
        k.ld(xt, xt[:], hin[t0:t0 + 512, :].rearrange("(t p) d -> p t d", p=128))
        mT = mT_ring.get()
        k.ld(mT, mT[:], mt_full[:, t0:t0 + 512].rearrange("(c p) s -> p c s", p=128))
        hm = hm_ring.get()
        for t in range(4):
            for half in range(2):
                ps = psP.get()
                for ch in range(8):
                    k.mm(ps, ps[:, :], mT, mT[:, ch, t * 128:(t + 1) * 128], wo, wo[:, ch, half * 512:(half + 1) * 512],
                         ch == 0, ch == 7)
                k.dve(lambda t=t, half=half, ps=ps: nc.vector.tensor_tensor(
                    hm[:, t, half * 512:(half + 1) * 512], ps[:, :], xt[:, t, half * 512:(half + 1) * 512], op=ALU.add),
                    [ps, xt], [hm])
        rms_stats(k, hm, 4, ss, rstd, junk)
        norm_transpose(k, c, hm, 4, rstd, gxc, xs, hnT, psT)
        for hh in range(4):
            ps = psP.get()
            for ch in range(8):
                k.mm(ps, ps[:, :], wq, wq[:, ch, hh * 128:(hh + 1) * 128], hnT, hnT[:, ch, :], ch == 0, ch == 7)
            k.dve(lambda hh=hh, ps=ps: nc.vector.tensor_copy(qT[:, hh, :], ps[:, :]), [ps], [qT])
        for hh in range(4):
            for mt in range(2):
                ps = psP.get()
                k.mm(ps, ps[:, :], KxT, KxT[:, hh, mt * 128:(mt + 1) * 128], qT, qT[:, hh, :], True, True)
                Eb = E_ring.get()
                k.act(Eb, Eb[:], ps, ps[:], AF.Exp, scale=scx)
                k.mm(psA, psA[:, :], Vx, Vx[:, mt, hh * 128:(hh + 1) * 128], Eb, Eb[:, :], mt == 0, mt == 1)
                k.mm(psL, psL[:, :], c["ones"], c["ones"][:, :], Eb, Eb[:, :], mt == 0, mt == 1)
            k.dve(lambda: nc.vector.reciprocal(rcp[:], psL[:, :]), [psL], [rcp])
            k.dve(lambda hh=hh: nc.vector.tensor_tensor(oT[:, hh, :], psA[:, :], rcp[:], op=ALU.mult), [psA, rcp], [oT])
        for t in range(4):
            for half in range(2):
                ps = psP.get()
                for hh in range(4):
                    k.mm(ps, ps[:, :], oT, oT[:, hh, t * 128:(t + 1) * 128], wxo_s, wxo_s[:, hh, half * 512:(half + 1) * 512],
                         hh == 0, hh == 3)
                k.dve(lambda t=t, half=half, ps=ps: nc.vector.tensor_tensor(
                    hm[:, t, half * 512:(half + 1) * 512], ps[:, :], hm[:, t, half * 512:(half + 1) * 512], op=ALU.add),
                    [ps, hm], [hm])
        if final:
            rms_stats(k, hm, 4, ss, rstd, junk)
            for t in range(4):
                k.dve(lambda t=t: nc.vector.scalar_tensor_tensor(hm[:, t, :], hm[:, t, :], rstd[:, t:t + 1], gf[:],
                                                                 op0=ALU.mult, op1=ALU.mult), [hm, rstd, gf], [hm])
        k.st(hout[t0:t0 + 512, :].rearrange("(t p) d -> p t d", p=128), hm, hm[:])
    fw.barrier()
    fw.release(mark)


def _consts():
    bf = ml_dtypes.bfloat16
    kk = np.arange(128)[:, None]
    qq = np.arange(128)[None, :]
    c = {}
    c["ident"] = np.eye(128, dtype=np.float32).astype(bf)
    c["maskle"] = np.where(kk <= qq, 0.0, -BIG).astype(np.float32).astype(bf)
    c["masklt"] = np.where(kk < qq, 0.0, -BIG).astype(np.float32).astype(bf)
    c["maskltp"] = np.where(kk < qq, 0.0, BIG).astype(np.float32).astype(bf)
    c["tricum"] = (kk >= qq).astype(np.float32).astype(bf)
    c["ones"] = np.ones((128, 128), np.float32).astype(bf)
    c["zeros"] = np.zeros((128, 128), np.float32).astype(bf)
    oh = np.zeros((32, 32, 128), np.float32)
    for n in range(32):
        oh[n, n, :] = 1.0
    c["onehot"] = oh.reshape(32, 32 * 128).astype(bf)
    s = np.zeros((65, 64), np.float32)
    s[64, :] = 1.0
    c["sel65"] = s
    c["ones64"] = np.full((64, 64), 1.0 / 64, np.float32)
    rc = np.zeros((128, 4), np.float32)
    r = np.arange(128)
    r64 = r % 64
    inv64 = (1.0 / (np.float32(THETA) ** (np.arange(0, 16, 2, dtype=np.float32) / np.float32(16)))).astype(np.float32)
    inv32 = (1.0 / (np.float32(THETA) ** (np.arange(0, 8, 2, dtype=np.float32) / np.float32(8)))).astype(np.float32)
    rc[:, 0] = np.where(r64 < 16, inv64[r64 % 8], 0.0)
    rc[:, 1] = np.where(r64 < 8, -1.0, 1.0)
    r32 = r % 32
    rc[:, 2] = np.where(r32 < 8, inv32[r32 % 4], 0.0)
    rc[:, 3] = np.where(r32 < 4, -1.0, 1.0)
    c["ropec"] = rc
    return c


CONST_SPECS = [("ident", [128, 128], BF16), ("maskle", [128, 128], BF16), ("masklt", [128, 128], BF16),
               ("maskltp", [128, 128], BF16), ("tricum", [128, 128], BF16), ("ones", [128, 128], BF16),
               ("zeros", [128, 128], BF16), ("onehot", [32, 4096], BF16), ("sel65", [65, 64], F32), ("ones64", [64, 64], F32),
               ("ropec", [128, 4], F32)]


def _partner64(cols):
    out = []
    for c0 in cols:
        idx = np.arange(64)
        p = idx.copy()
        p[0:8] = idx[8:16]
        p[8:16] = idx[0:8]
        out.append(c0 + p)
    return np.concatenate(out)


def _partner32(cols):
    out = []
    for c0 in cols:
        idx = np.arange(32)
        p = idx.copy()
        p[0:4] = idx[4:8]
        p[4:8] = idx[0:4]
        out.append(c0 + p)
    return np.concatenate(out)


def _win_cols(g):
    MQ, MK, MV, MG, DQ, DK, DV, DG, SQ, SK, SV, SG = 0, 384, 768, 1152, 1536, 1792, 2048, 2304, 2560, 2944, 3328, 3712
    hm = [3 * g, 3 * g + 1, 3 * g + 2]
    hd = [2 * g, 2 * g + 1]
    r64 = np.arange(64)
    def blk(base, h, w=64):
        return base + h * w + np.arange(w)
    groups = []
    groups.append(np.concatenate([blk(MQ, hm[0]), blk(MQ, hm[1])]))
    groups.append(np.concatenate([blk(MK, hm[0]), blk(MK, hm[1])]))
    groups.append(np.concatenate([blk(MQ, hm[2]), blk(MK, hm[2])]))
    groups.append(_partner64([MQ + 64 * hm[0], MQ + 64 * hm[1]]))
    groups.append(_partner64([MK + 64 * hm[0], MK + 64 * hm[1]]))
    groups.append(_partner64([MQ + 64 * hm[2], MK + 64 * hm[2]]))
    groups.append(np.concatenate([blk(DQ, hd[0]), blk(DQ, hd[1])]))
    groups.append(np.concatenate([blk(DK, hd[0]), blk(DK, hd[1])]))
    groups.append(_partner32([DQ + 64 * hd[0], DQ + 64 * hd[0] + 32, DQ + 64 * hd[1], DQ + 64 * hd[1] + 32]))
    groups.append(_partner32([DK + 64 * hd[0], DK + 64 * hd[0] + 32, DK + 64 * hd[1], DK + 64 * hd[1] + 32]))
    groups.append(np.concatenate([blk(SQ, hm[0]), blk(SQ, hm[1])]))
    groups.append(np.concatenate([blk(SK, hm[0]), blk(SK, hm[1])]))
    groups.append(np.concatenate([blk(SQ, hm[2]), blk(SK, hm[2])]))
    gates = np.concatenate([blk(MG, hm[0]), blk(MG, hm[1]), blk(MG, hm[2]), blk(DG, hd[0]), blk(DG, hd[1]),
                            blk(SG, hm[0]), blk(SG, hm[1]), blk(SG, hm[2])])
    for i in range(4):
        groups.append(gates[i * 128:(i + 1) * 128])
    vcols = np.concatenate([blk(MV, hm[0]), blk(MV, hm[1]), blk(MV, hm[2]), blk(DV, hd[0]), blk(DV, hd[1]),
                            blk(SV, hm[0]), blk(SV, hm[1]), blk(SV, hm[2])])
    return np.concatenate(groups + [vcols])


def _merged_rows(g):
    return np.concatenate([np.arange(192 * g, 192 * g + 192), 384 + np.arange(128 * g, 128 * g + 128),
                           640 + np.arange(192 * g, 192 * g + 192)])


def _gcols(gvec):
    return np.ascontiguousarray(np.asarray(gvec, np.float32).reshape(8, 128).T)


def build_a(S, lam_init, debug=False):
    nc = bass.Bass("TRN2", target_bir_lowering=False)
    k = K(nc, S)
    cin = {n: nc.dram_tensor("c_" + n, s, d, kind="ExternalInput").ap() for n, s, d in CONST_SPECS}
    hin = nc.dram_tensor("hin", [S, D], F32, kind="ExternalInput").ap()
    pos = nc.dram_tensor("pos", [1, S], I32, kind="ExternalInput").ap()
    win = nc.dram_tensor("win", [D, NCOL], F32, kind="ExternalInput").ap()
    gin = nc.dram_tensor("gin", [128, 8], F32, kind="ExternalInput").ap()
    lamin = nc.dram_tensor("lamin", [1, 128], F32, kind="ExternalInput").ap()
    hng = nc.dram_tensor("hng", [64, 1], F32, kind="ExternalInput").ap()
    mt = nc.dram_tensor("mt", [512, S], BF16, kind="ExternalOutput").ap()
    kd = "ExternalOutput" if debug else "Internal"
    rope_d = nc.dram_tensor("rope_d", [4, 128, S], F32, kind=kd).ap()
    qk_d = nc.dram_tensor("qk_d", [QKROWS, S], BF16, kind=kd).ap()
    gt_d = nc.dram_tensor("gt_d", [512, S], BF16, kind=kd).ap()
    v_d = nc.dram_tensor("v_d", [S, 512], BF16, kind=kd).ap()
    rope_b, qk_b, gt_b, v_b, mt_b = Buf(None, "rope"), Buf(None, "qk"), Buf(None, "gt"), Buf(None, "v"), Buf(None, "mt")
    c = load_consts(k, cin)
    build_rope_tables(k, c, pos, rope_d, rope_b)
    stage_a_proj(k, c, hin, win, gin, rope_d, rope_b, qk_d, qk_b, gt_d, gt_b, v_d, v_b)
    stage_a_attn(k, c, lamin, hng, lam_init, qk_d, qk_b, gt_d, gt_b, v_d, v_b, mt, mt_b)
    k.fw.finish()
    k.fw.release(0)
    return nc


def build_b(ST, final):
    nc = bass.Bass("TRN2", target_bir_lowering=False)
    k = K(nc, ST)
    cin = {n: nc.dram_tensor("c_" + n, s, d, kind="ExternalInput").ap() for n, s, d in CONST_SPECS}
    hin = nc.dram_tensor("hin", [ST, D], F32, kind="ExternalInput").ap()
    mtf = nc.dram_tensor("mtf", [1024, ST], BF16, kind="ExternalInput").ap()
    wout = nc.dram_tensor("wout", [D, D], F32, kind="ExternalInput").ap()
    gx = nc.dram_tensor("gx", [128, 8], F32, kind="ExternalInput").ap()
    memin = nc.dram_tensor("memin", [MEM, D], F32, kind="ExternalInput").ap()
    gm = nc.dram_tensor("gm", [128, 8], F32, kind="ExternalInput").ap()
    wxq = nc.dram_tensor("wxq", [D, 512], F32, kind="ExternalInput").ap()
    wxkv = nc.dram_tensor("wxkv", [D, 1024], F32, kind="ExternalInput").ap()
    wxo = nc.dram_tensor("wxo", [512, D], F32, kind="ExternalInput").ap()
    gfin = nc.dram_tensor("gfin", [1, D], F32, kind="ExternalInput").ap()
    hout = nc.dram_tensor("hout", [ST, D], F32, kind="ExternalOutput").ap()
    c = load_consts(k, cin)
    stage_b(k, c, ST, hin, mtf, wout, gx, memin, gm, wxq, wxkv, wxo, gfin, hout, final)
    k.fw.finish()
    k.fw.release(0)
    return nc


def build_fused(S, depth=2):
    nc = bass.Bass("TRN2", target_bir_lowering=False)
    k = K(nc, S)
    cin = {n: nc.dram_tensor("c_" + n, s, d, kind="ExternalInput").ap() for n, s, d in CONST_SPECS}
    x = nc.dram_tensor("x", [S, D], F32, kind="ExternalInput").ap()
    pos = nc.dram_tensor("pos", [1, S], I32, kind="ExternalInput").ap()
    memin = nc.dram_tensor("memin", [MEM, D], F32, kind="ExternalInput").ap()
    gfin = nc.dram_tensor("gfin", [1, D], F32, kind="ExternalInput").ap()
    L = []
    for l in range(depth):
        d_ = {}
        for g in range(2):
            d_["win%d" % g] = nc.dram_tensor("win_%d_%d" % (l, g), [D, NCOL], F32, kind="ExternalInput").ap()
        d_["gin"] = nc.dram_tensor("gin_%d" % l, [128, 8], F32, kind="ExternalInput").ap()
        d_["lamin"] = nc.dram_tensor("lamin_%d" % l, [1, 128], F32, kind="ExternalInput").ap()
        d_["hng"] = nc.dram_tensor("hng_%d" % l, [64, 1], F32, kind="ExternalInput").ap()
        d_["wout"] = nc.dram_tensor("wout_%d" % l, [D, D], F32, kind="ExternalInput").ap()
        d_["gx"] = nc.dram_tensor("gx_%d" % l, [128, 8], F32, kind="ExternalInput").ap()
        d_["gm"] = nc.dram_tensor("gm_%d" % l, [128, 8], F32, kind="ExternalInput").ap()
        d_["wxq"] = nc.dram_tensor("wxq_%d" % l, [D, 512], F32, kind="ExternalInput").ap()
        d_["wxkv"] = nc.dram_tensor("wxkv_%d" % l, [D, 1024], F32, kind="ExternalInput").ap()
        d_["wxo"] = nc.dram_tensor("wxo_%d" % l, [512, D], F32, kind="ExternalInput").ap()
        L.append(d_)
    out = nc.dram_tensor("out", [S, D], F32, kind="ExternalOutput").ap()
    h1 = nc.dram_tensor("h1_d", [S, D], F32, kind="Internal").ap()
    mt = nc.dram_tensor("mt_d", [1024, S], BF16, kind="Internal").ap()
    rope_d = nc.dram_tensor("rope_d", [4, 128, S], F32, kind="Internal").ap()
    qk_d = nc.dram_tensor("qk_d", [QKROWS, S], BF16, kind="Internal").ap()
    gt_d = nc.dram_tensor("gt_d", [512, S], BF16, kind="Internal").ap()
    v_d = nc.dram_tensor("v_d", [S, 512], BF16, kind="Internal").ap()
    rope_b, qk_b, gt_b, v_b, mt_b = Buf(None, "rope"), Buf(None, "qk"), Buf(None, "gt"), Buf(None, "v"), Buf(None, "mt")
    c = load_consts(k, cin)
    build_rope_tables(k, c, pos, rope_d, rope_b)
    hin = x
    for l in range(depth):
        lam_init = 0.8 - 0.6 * math.exp(-0.3 * l)
        for g in range(2):
            stage_a_proj(k, c, hin, L[l]["win%d" % g], L[l]["gin"], rope_d, rope_b, qk_d, qk_b, gt_d, gt_b, v_d, v_b)
            stage_a_attn(k, c, L[l]["lamin"], L[l]["hng"], lam_init, qk_d, qk_b, gt_d, gt_b, v_d, v_b, mt, mt_b, mrow0=512 * g)
        final = (l == depth - 1)
        hout = out if final else h1
        stage_b(k, c, S, hin, mt, L[l]["wout"], L[l]["gx"], memin, L[l]["gm"], L[l]["wxq"], L[l]["wxkv"], L[l]["wxo"],
                gfin, hout, final)
        hin = h1
    k.fw.finish()
    k.fw.release(0)
    return nc

_CACHE = {}


def _get(key, fn):
    if key not in _CACHE:
        _CACHE[key] = fn()
    return _CACHE[key]


def kernel_unfused(x, mem, positions, attn_norm_g, w_in, w_out, diff_lambda_q1, diff_lambda_k1, diff_lambda_q2,
           diff_lambda_k2, diff_head_norm_g, xattn_norm_g, mem_norm_g, w_xq, w_xkv, w_xo, final_norm_g):
    x = np.asarray(x, np.float32)
    B, S, _ = x.shape
    consts = _consts()
    cmap = {"c_" + n: v for n, v in consts.items()}
    ncores = 2 * B
    SH = S // 2
    h = [np.ascontiguousarray(x[b]) for b in range(B)]
    depth = np.asarray(w_in).shape[0]
    for l in range(depth):
        lam_init = 0.8 - 0.6 * math.exp(-0.3 * l)
        nca = _get(("a", S, l), lambda: build_a(S, lam_init))
        in_maps = []
        for cidx in range(ncores):
            b, g = cidx // 2, cidx % 2
            lamin = np.concatenate([np.asarray(diff_lambda_q1[l]), np.asarray(diff_lambda_k1[l]),
                                    np.asarray(diff_lambda_q2[l]), np.asarray(diff_lambda_k2[l])]).astype(np.float32)[None, :]
            m = {"hin": h[b], "pos": np.ascontiguousarray(np.asarray(positions)[b:b + 1].astype(np.int32)),
                 "win": np.ascontiguousarray(np.asarray(w_in[l], np.float32)[:, _win_cols(g)]),
                 "gin": _gcols(attn_norm_g[l]), "lamin": lamin,
                 "hng": np.ascontiguousarray(np.asarray(diff_head_norm_g[l], np.float32).reshape(64, 1))}
            m.update(cmap)
            in_maps.append(m)
        res = run_bass_kernel_spmd(nca, in_maps, core_ids=list(range(ncores)))
        mts = [r["mt"] for r in res.results]
        final = (l == depth - 1)
        ncb = _get(("b", SH, final), lambda: build_b(SH, final))
        rows = np.concatenate([_merged_rows(0), _merged_rows(1)])
        wout_p = np.ascontiguousarray(np.asarray(w_out[l], np.float32)[rows, :])
        in_maps = []
        for cidx in range(ncores):
            b, g = cidx // 2, cidx % 2
            mtf = np.ascontiguousarray(np.concatenate([mts[2 * b][:, g * SH:(g + 1) * SH],
                                                       mts[2 * b + 1][:, g * SH:(g + 1) * SH]], axis=0))
            m = {"hin": np.ascontiguousarray(h[b][g * SH:(g + 1) * SH]), "mtf": mtf, "wout": wout_p,
                 "gx": _gcols(xattn_norm_g[l]), "memin": np.ascontiguousarray(np.asarray(mem[b], np.float32)),
                 "gm": _gcols(mem_norm_g[l]), "wxq": np.ascontiguousarray(np.asarray(w_xq[l], np.float32)),
                 "wxkv": np.ascontiguousarray(np.asarray(w_xkv[l], np.float32)),
                 "wxo": np.ascontiguousarray(np.asarray(w_xo[l], np.float32)),
                 "gfin": np.ascontiguousarray(np.asarray(final_norm_g, np.float32)[None, :])}
            m.update(cmap)
            in_maps.append(m)
        res = run_bass_kernel_spmd(ncb, in_maps, core_ids=list(range(ncores)))
        h = [np.concatenate([res.results[2 * b]["hout"], res.results[2 * b + 1]["hout"]], axis=0) for b in range(B)]
    return np.stack(h, axis=0).astype(np.float32)


def kernel(x, mem, positions, attn_norm_g, w_in, w_out, diff_lambda_q1, diff_lambda_k1, diff_lambda_q2,
           diff_lambda_k2, diff_head_norm_g, xattn_norm_g, mem_norm_g, w_xq, w_xkv, w_xo, final_norm_g):
    x = np.asarray(x, np.float32)
    B, S, _ = x.shape
    depth = np.asarray(w_in).shape[0]
    consts = _consts()
    base = {"c_" + n: v for n, v in consts.items()}
    base["gfin"] = np.ascontiguousarray(np.asarray(final_norm_g, np.float32)[None, :])
    rows = np.concatenate([_merged_rows(0), _merged_rows(1)])
    for l in range(depth):
        wl = np.asarray(w_in[l], np.float32)
        for g in range(2):
            base["win_%d_%d" % (l, g)] = np.ascontiguousarray(wl[:, _win_cols(g)])
        base["gin_%d" % l] = _gcols(attn_norm_g[l])
        base["lamin_%d" % l] = np.concatenate([np.asarray(diff_lambda_q1[l]), np.asarray(diff_lambda_k1[l]),
                                               np.asarray(diff_lambda_q2[l]), np.asarray(diff_lambda_k2[l])]).astype(np.float32)[None, :]
        base["hng_%d" % l] = np.ascontiguousarray(np.asarray(diff_head_norm_g[l], np.float32).reshape(64, 1))
        base["wout_%d" % l] = np.ascontiguousarray(np.asarray(w_out[l], np.float32)[rows, :])
        base["gx_%d" % l] = _gcols(xattn_norm_g[l])
        base["gm_%d" % l] = _gcols(mem_norm_g[l])
        base["wxq_%d" % l] = np.ascontiguousarray(np.asarray(w_xq[l], np.float32))
        base["wxkv_%d" % l] = np.ascontiguousarray(np.asarray(w_xkv[l], np.float32))
        base["wxo_%d" % l] = np.ascontiguousarray(np.asarray(w_xo[l], np.float32))
    ncores = 2 * B
    in_maps = []
    for cidx in range(ncores):
        b = cidx // 2
        m = dict(base)
        m["x"] = np.ascontiguousarray(x[b])
        m["pos"] = np.ascontiguousarray(np.asarray(positions)[b:b + 1].astype(np.int32))
        m["memin"] = np.ascontiguousarray(np.asarray(mem[b], np.float32))
        in_maps.append(m)
    nc = _get(("fused", S, depth), lambda: build_fused(S, depth))
    res = run_bass_kernel_spmd(nc, in_maps, core_ids=list(range(ncores)))
    SH = S // 2
    outs = [np.concatenate([res.results[2 * b]["out"][:SH], res.results[2 * b + 1]["out"][SH:]], axis=0) for b in range(B)]
    return np.stack(outs, axis=0).astype(np.float32)
```
